# Optimizing a Trainium2 kernel written in Bass

```python
import jax, jax.numpy as jnp
from jax import lax
import numpy as np

D_MODEL = 1024
BATCH = 4
SEQ = 4096
DEPTH = 1

GRID_W = 64
HEAD_DIM = 64
N_NA_HEADS = D_MODEL // 128
NA_WIDTH = N_NA_HEADS * HEAD_DIM
WIN_H = 8
WIN_W = 16
N_FOURIER_GROUPS = 4
FOURIER_WIDTH = D_MODEL // 2
FOURIER_GROUP_DIM = FOURIER_WIDTH // N_FOURIER_GROUPS
IN_WIDTH = 3 * NA_WIDTH + FOURIER_WIDTH + 2 * D_MODEL
D_FF = 4 * D_MODEL
N_MOD = 6
EPS = 1e-6

kernel_name = "hybrid_natten_fnet_gated_block"


def _rmsnorm(x, w):
    xf = x.astype(jnp.float32)
    y = xf * lax.rsqrt(jnp.mean(xf * xf, axis=-1, keepdims=True) + EPS)
    return (y * w.astype(jnp.float32)).astype(x.dtype)


def _neighbourhood_attention(q, k, v, rpb):
    b, s, h, dh = q.shape
    rows = s // GRID_W
    kh = min(WIN_H, rows)
    kw = min(WIN_W, GRID_W)
    q = q.reshape(b, rows, GRID_W, h, dh)
    k = k.reshape(b, rows, GRID_W, h, dh)
    v = v.reshape(b, rows, GRID_W, h, dh)
    cols = np.arange(GRID_W)
    col_start = np.clip(cols - kw // 2, 0, GRID_W - kw)
    col_idx = col_start[:, None] + np.arange(kw)[None, :]
    col_bias_idx = col_idx - cols[:, None] + (WIN_W - 1)
    scale = dh ** -0.5

    def row_block(r):
        rs = jnp.clip(r - kh // 2, 0, rows - kh)
        k_rows = lax.dynamic_slice_in_dim(k, rs, kh, axis=1)
        v_rows = lax.dynamic_slice_in_dim(v, rs, kh, axis=1)
        k_g = k_rows[:, :, col_idx]
        v_g = v_rows[:, :, col_idx]
        q_r = lax.dynamic_index_in_dim(q, r, axis=1, keepdims=False)
        scores = jnp.einsum('bqhd,biqjhd->bhqij', q_r, k_g).astype(jnp.float32) * scale
        row_bias_idx = rs + jnp.arange(kh) - r + (WIN_H - 1)
        bias = rpb[:, row_bias_idx[None, :, None], col_bias_idx[:, None, :]]
        scores = scores + bias.astype(jnp.float32)[None]
        p = jax.nn.softmax(scores.reshape(b, h, GRID_W, kh * kw), axis=-1)
        p = p.reshape(b, h, GRID_W, kh, kw).astype(v.dtype)
        return jnp.einsum('bhqij,biqjhd->bqhd', p, v_g)

    out = lax.map(row_block, jnp.arange(rows))
    return out.transpose(1, 0, 2, 3, 4).reshape(b, s, h * dh)


def _fourier_mix(u):
    b, s, _ = u.shape
    uf = u.astype(jnp.float32).reshape(b, s, N_FOURIER_GROUPS, FOURIER_GROUP_DIM)
    y = jnp.fft.fft2(uf, axes=(1, 3), norm="ortho").real
    return y.reshape(b, s, FOURIER_WIDTH).astype(u.dtype)


def setup_inputs(seed: int = 0) -> dict:
    key = jax.random.key(seed)
    ks = jax.random.split(key, 18)
    f32 = jnp.float32

    def nrm(k, shape, fan_in):
        return jax.random.normal(k, shape, f32) * (fan_in ** -0.5)

    L = DEPTH
    return {
        "x": jax.random.normal(ks[0], (BATCH, SEQ, D_MODEL), f32),
        "c": jax.random.normal(ks[1], (BATCH, D_MODEL), f32),
        "norm1_w": 1.0 + 0.05 * jax.random.normal(ks[2], (L, D_MODEL), f32),
        "norm2_w": 1.0 + 0.05 * jax.random.normal(ks[3], (L, D_MODEL), f32),
        "w_ada": nrm(ks[4], (L, D_MODEL, N_MOD * D_MODEL), D_MODEL) * 0.5,
        "b_ada": 0.02 * jax.random.normal(ks[5], (L, N_MOD * D_MODEL), f32),
        "w_in": nrm(ks[6], (L, D_MODEL, IN_WIDTH), D_MODEL),
        "b_in": 0.02 * jax.random.normal(ks[7], (L, IN_WIDTH), f32),
        "q_norm_w": 1.0 + 0.05 * jax.random.normal(ks[8], (L, HEAD_DIM), f32),
        "k_norm_w": 1.0 + 0.05 * jax.random.normal(ks[9], (L, HEAD_DIM), f32),
        "rpb": 0.1 * jax.random.normal(ks[10], (L, N_NA_HEADS, 2 * WIN_H - 1, 2 * WIN_W - 1), f32),
        "w_na_out": nrm(ks[11], (L, NA_WIDTH, D_MODEL), NA_WIDTH),
        "w_fn_out": nrm(ks[12], (L, FOURIER_WIDTH, D_MODEL), FOURIER_WIDTH),
        "b_fn_out": 0.02 * jax.random.normal(ks[13], (L, D_MODEL), f32),
        "w_o": nrm(ks[14], (L, D_MODEL, D_MODEL), D_MODEL),
        "w_mlp_in": nrm(ks[15], (L, D_MODEL, D_FF), D_MODEL),
        "w_mlp_out": nrm(ks[16], (L, D_FF, D_MODEL), D_FF),
    }


def reference(x, c, norm1_w, norm2_w, w_ada, b_ada, w_in, b_in, q_norm_w, k_norm_w, rpb,
              w_na_out, w_fn_out, b_fn_out, w_o, w_mlp_in, w_mlp_out):
    b, s, d = x.shape
    split_pts = [NA_WIDTH, 2 * NA_WIDTH, 3 * NA_WIDTH, 3 * NA_WIDTH + FOURIER_WIDTH,
                 3 * NA_WIDTH + FOURIER_WIDTH + D_MODEL]
    for l in range(DEPTH):
        mod = jax.nn.silu(c) @ w_ada[l] + b_ada[l]
        shift1, scale1, gate1, shift2, scale2, gate2 = [m[:, None, :] for m in jnp.split(mod, N_MOD, axis=-1)]

        h = _rmsnorm(x, norm1_w[l]) * (1.0 + scale1) + shift1
        z = h @ w_in[l] + b_in[l]
        q, k, v, u, g_a, g_b = jnp.split(z, split_pts, axis=-1)
        q = _rmsnorm(q.reshape(b, s, N_NA_HEADS, HEAD_DIM), q_norm_w[l])
        k = _rmsnorm(k.reshape(b, s, N_NA_HEADS, HEAD_DIM), k_norm_w[l])
        v = v.reshape(b, s, N_NA_HEADS, HEAD_DIM)
        y_na = _neighbourhood_attention(q, k, v, rpb[l]) @ w_na_out[l]
        y_fn = _fourier_mix(u) @ w_fn_out[l] + b_fn_out[l]
        merged = jax.nn.sigmoid(g_a) * y_na + jax.nn.sigmoid(g_b) * y_fn
        x = x + gate1 * (merged @ w_o[l])

        h2 = _rmsnorm(x, norm2_w[l]) * (1.0 + scale2) + shift2
        x = x + gate2 * (jnp.square(jax.nn.relu(h2 @ w_mlp_in[l])) @ w_mlp_out[l])
    return x
```

```python
import contextlib
import numpy as np
import ml_dtypes
import concourse.bass as bass
import concourse.mybir as mybir
from concourse.bass_utils import run_bass_kernel_spmd

F32 = mybir.dt.float32
BF16 = mybir.dt.bfloat16
AF = mybir.ActivationFunctionType
ALU = mybir.AluOpType
AX = mybir.AxisListType
EPS = 1e-6
KB = 1024
ARENA_BYTES = 206 * KB
NEG = -30000.0


class Tile:
    __slots__ = ("name", "w", "rs")

    def __init__(self, name=""):
        self.name = name
        self.w = None
        self.rs = []


class Op:
    __slots__ = ("eng", "fn", "deps", "dma", "sig", "sem", "val", "prev_dma")

    def __init__(self, eng, fn, deps, dma):
        self.eng = eng
        self.fn = fn
        self.deps = deps
        self.dma = dma
        self.sig = False
        self.sem = None
        self.val = 0
        self.prev_dma = None


class Sched:
    ENGS = ("pe", "act", "dve", "pool", "sp")

    def __init__(self, n_dma_sems=16, same_engine_sync=True):
        self.q = {e: [] for e in self.ENGS}
        self.n_dma_sems = n_dma_sems
        self.same_engine_sync = same_engine_sync
        self.dma_ops = []
        self.dma_since_barrier = []
        self.dma_carry = []
        self.last_real = {}

    def op(self, eng, fn, reads=(), writes=(), dma=False, extra_deps=(), nobarrier=False):
        deps = set(extra_deps)
        for t in reads:
            if t.w is not None:
                deps.add(t.w)
        for t in writes:
            if t.w is not None:
                deps.add(t.w)
            deps.update(t.rs)
        o = Op(eng, fn, deps, dma)
        for t in reads:
            t.rs.append(o)
        for t in writes:
            t.w = o
            t.rs = []
        self.q[eng].append(o)
        if fn is not None and not dma:
            self.last_real[eng] = o
        if dma:
            self.dma_ops.append(o)
            if not nobarrier:
                self.dma_since_barrier.append(o)
            else:
                self.dma_carry.append(o)
        return o

    def barrier(self):
        deps = set(self.last_real.values()) | set(self.dma_since_barrier)
        self.dma_since_barrier = list(self.dma_carry)
        self.dma_carry = []
        for e in self.ENGS:
            self.op(e, None, extra_deps=list(deps))

    def _skip(self, d, o):
        if d.dma:
            return False
        if d.eng == "pe" and o.eng == "pe":
            return True
        if (not self.same_engine_sync) and d.eng == o.eng:
            return True
        return False

    def finalize(self, sems):
        for e in self.ENGS:
            for o in self.q[e]:
                for d in o.deps:
                    if not self._skip(d, o):
                        d.sig = True
        for e in self.ENGS:
            cnt = 0
            for o in self.q[e]:
                if o.dma or o.fn is None:
                    continue
                if o.sig:
                    cnt += 1
                    o.sem = sems[e]
                    o.val = cnt
        for qn in ("sp", "pool", "act"):
            pool = sems["dma_" + qn]
            dcount = [0] * len(pool)
            dlast = [None] * len(pool)
            i = 0
            for o in self.dma_ops:
                if o.eng != qn:
                    continue
                s = i % len(pool)
                i += 1
                dcount[s] += 16
                o.sem = pool[s]
                o.val = dcount[s]
                o.prev_dma = dlast[s]
                dlast[s] = o
                o.sig = True

    def replay(self, eng_name, eng):
        seen = {}

        def wait(d):
            if d.sem is None:
                return
            k = id(d.sem)
            if seen.get(k, 0) >= d.val:
                return
            eng.wait_ge(d.sem, d.val)
            seen[k] = d.val

        for o in self.q[eng_name]:
            for d in sorted(o.deps, key=lambda d: d.val):
                if self._skip(d, o):
                    continue
                wait(d)
            if o.dma and o.prev_dma is not None:
                wait(o.prev_dma)
            if o.fn is None:
                continue
            ins = o.fn(eng)
            if o.sig:
                ins.then_inc(o.sem, 16 if o.dma else 1)


def build_program(stop_after=None, same_engine_sync=True):
    nc = bass.Bass("TRN2", target_bir_lowering=False)

    def din(name, shape, dt=F32):
        return nc.dram_tensor(name, list(shape), dt, kind="ExternalInput").ap()

    x_d = din("x", [4096, 1024])
    c_d = din("c_pp", [128, 8])
    wada_d = din("w_ada", [1024, 6144])
    bada_d = din("b_ada", [1, 6144])
    badapp_d = din("bada_pp", [128, 48])
    n1w_d = din("n1w_pp", [128, 8])
    n2w_d = din("n2w_pp", [128, 8])
    win_d = din("w_in", [1024, 4096])
    bin_d = din("bin_pp", [128, 32])
    qkw_d = din("qkw", [128, 2])
    btab_d = din("btab", [128, 8 * 3 * 640])
    wna_d = din("w_na", [512, 1024])
    wfn_d = din("w_fn", [512, 1024])
    bfn_d = din("bfn_pp", [128, 8])
    wo_d = din("w_o", [1024, 1024])
    w1_d = din("w1", [1024, 4096])
    w2_d = din("w2", [4096, 1024])
    cid_d = din("cst_ident", [128, 128])
    cbo_d = din("cst_bones", [128, 128])
    ccs_d = din("cst_cs", [128, 256])
    dft_d = din("dft", [32, 128, 4096], BF16)
    out_d = nc.dram_tensor("out", [2048, 1024], F32, kind="ExternalOutput").ap()

    es = contextlib.ExitStack()
    with es:
        arena = es.enter_context(nc.sbuf_tensor("arena", [128, ARENA_BYTES // 2], BF16))
        PS_all = es.enter_context(nc.psum_tensor("ps_all", [128, 4096], F32))
        PS = [PS_all[:, i * 1024:(i + 1) * 1024] for i in range(4)]
        sems = {e: es.enter_context(nc.semaphore(f"s_{e}")) for e in Sched.ENGS}
        sems["dma_sp"] = [es.enter_context(nc.semaphore(f"sdsp{i}")) for i in range(12)]
        sems["dma_pool"] = [es.enter_context(nc.semaphore(f"sdpl{i}")) for i in range(8)]
        sems["dma_act"] = [es.enter_context(nc.semaphore(f"sdac{i}")) for i in range(4)]
        S = Sched(n_dma_sems=16, same_engine_sync=same_engine_sync)

        def V(off, shape, dt):
            n = int(np.prod(shape[1:]))
            esz = 2 if dt == BF16 else 4
            assert off % 32 == 0 or n * esz < 32 or off % 4 == 0
            assert off + n * esz <= ARENA_BYTES, (off, shape)
            v = arena[0:shape[0], off // 2:(off + n * esz) // 2]
            if dt == F32:
                v = v.bitcast(F32)
            if len(shape) == 3:
                v = v.rearrange("p (a b) -> p a b", a=shape[1])
            elif len(shape) == 4:
                v = v.rearrange("p (a b c) -> p a b c", a=shape[1], b=shape[2])
            return v

        def bank(k):
            return PS[k // 2][:, (k % 2) * 512:(k % 2) * 512 + 512]

        def bank_bf(k):
            return bank(k).bitcast(BF16)

        def bc(ap, shape):
            return ap.to_broadcast(list(shape))

        ident = V(0, [128, 128], BF16)
        identf = V(256, [128, 128], F32)
        bones = V(768, [128, 128], BF16)
        cs_t = V(1024, [128, 256], BF16)
        sm = 1536
        c_sb = V(sm + 0, [128, 8], F32)
        sc_sb = V(sm + 32, [128, 8], F32)
        n1w = V(sm + 64, [128, 8], F32)
        n2w = V(sm + 96, [128, 8], F32)
        gs1 = V(sm + 128, [128, 8], F32)
        sh1 = V(sm + 160, [128, 8], F32)
        gs2 = V(sm + 192, [128, 8], F32)
        sh2 = V(sm + 224, [128, 8], F32)
        bin_pp = V(sm + 256, [128, 32], F32)
        bfn_pp = V(sm + 384, [128, 8], F32)
        qkw = V(sm + 416, [128, 2], F32)
        kw8 = V(sm + 424, [128, 1], F32)
        ppx = V(sm + 448, [128, 32], F32)
        bada_pp = V(sm + 576, [128, 48], F32)
        ones_row = V(2432, [1, 128], F32)
        gate1_bc = V(4 * KB, [128, 1024], F32)
        gate2_bc = V(8 * KB, [128, 1024], F32)
        stat = V(12 * KB, [128, 256], F32)
        t_const = Tile("const")
        t_mods = Tile("mods")
        t_gate = Tile("gate")

        def dma(eng, out, in_, reads=(), writes=(), extra_deps=(), nobarrier=False):
            return S.op(eng, lambda e: e.dma_start(out=out, in_=in_), reads=reads, writes=writes, dma=True,
                        extra_deps=extra_deps, nobarrier=nobarrier)

        t_cs = []

        def cdma(eng, out, in_):
            t = Tile()
            t_cs.append(t)
            dma(eng, out, in_, writes=[t])

        cdma("sp", c_sb, c_d)
        cdma("pool", ident, cid_d)
        cdma("sp", identf, cid_d)
        cdma("sp", bada_pp, badapp_d)
        cdma("sp", n1w, n1w_d)
        cdma("sp", n2w, n2w_d)
        cdma("sp", bin_pp, bin_d)
        cdma("sp", bfn_pp, bfn_d)
        cdma("sp", qkw, qkw_d)
        cdma("pool", bones, cbo_d)
        cdma("pool", cs_t, ccs_d)
        S.op("dve", lambda e: e.memset(stat[:, 208:209], 0.0), reads=t_cs, writes=[t_const])
        S.op("dve", lambda e: e.memset(ones_row, 1.0), writes=[t_const])
        S.op("dve", lambda e: e.tensor_scalar(out=kw8, in0=qkw[:, 1:2], scalar1=8.0, scalar2=None, op0=ALU.mult),
             reads=[t_const], writes=[t_const])

        wada_buf = [V(110 * KB + i * 8 * KB, [128, 8, 512], BF16) for i in range(3)]
        t_wada = [Tile(), Tile(), Tile()]
        screp = V(134 * KB, [128, 8, 128], BF16)
        modrows = V(136 * KB, [128, 2048], F32)
        tmpdiag = V(184 * KB, [128, 8, 128], F32)
        t_screp, t_modrows, t_tmpdiag = Tile(), Tile(), Tile()
        t_bank = [Tile(f"bank{k}") for k in range(8)]
        Wqkvu = V(14 * KB, [128, 8, 2048], BF16)
        t_W = [Tile("Wqkv_a"), Tile("Wqkv_d")]
        t_Wu = Tile("Wu")
        win_v = win_d.rearrange("(kc p) n -> p kc n", p=128)
        wada_v = wada_d.rearrange("(kc p) n -> p kc n", p=128)

        S.op("act", lambda e: e.activation(out=sc_sb, in_=c_sb, func=AF.Silu), reads=[t_const], writes=[t_screp])
        S.op("dve", lambda e: e.tensor_copy(out=screp, in_=bc(sc_sb.unsqueeze(2), [128, 8, 128])),
             reads=[t_screp], writes=[t_screp])
        dma("sp", gate1_bc, bada_d[:, 2048:3072].partition_broadcast(128), writes=[t_gate])
        dma("sp", gate2_bc, bada_d[:, 5120:6144].partition_broadcast(128), writes=[t_gate])
        for nb in range(4):
            b = nb % 3
            dma("pool", wada_buf[b], wada_v[:, :, nb * 512:(nb + 1) * 512], writes=[t_wada[b]])

            def mm(e, nb=nb, b=b):
                ps = bank(nb % 2)
                ins = None
                for kc in range(8):
                    ins = e.matmul(ps, lhsT=screp[:, kc, :], rhs=wada_buf[b][:, kc, :], start=(kc == 0),
                                   stop=(kc == 7))
                return ins
            S.op("pe", mm, reads=[t_screp, t_wada[b]], writes=[t_bank[nb % 2]])
            S.op("act", lambda e, nb=nb: e.activation(out=modrows[:, nb * 512:(nb + 1) * 512], in_=bank(nb % 2),
                                                      func=AF.Copy),
                 reads=[t_bank[nb % 2]], writes=[t_modrows])
        for kc in range(8):
            dma("pool", Wqkvu[:, kc, 1536:2048], win_v[:, kc, 1536:2048], writes=[t_Wu], nobarrier=True)
        stg_w = [V(46 * KB + kc * 6 * KB, [128, 1536], F32) for kc in range(8)]
        t_stgw = [Tile() for _ in range(8)]
        for kc in range(8):
            dma("sp", stg_w[kc], win_v[:, kc, 0:1536], writes=[t_stgw[kc]], nobarrier=True)
        for i in range(2):
            src = modrows[:, i * 1024:(i + 1) * 1024].rearrange("p (a b) -> p a b", a=8)
            S.op("dve", lambda e, src=src: e.tensor_tensor(out=tmpdiag, in0=src,
                                                            in1=bc(identf.unsqueeze(1), [128, 8, 128]), op=ALU.mult),
                 reads=[t_modrows, t_const], writes=[t_tmpdiag])
            S.op("dve", lambda e, i=i: e.tensor_reduce(out=ppx[:, i * 8:(i + 1) * 8], in_=tmpdiag, axis=AX.X,
                                                       op=ALU.add),
                 reads=[t_tmpdiag], writes=[t_mods])
            S.op("dve", lambda e, i=i: e.tensor_tensor(out=ppx[:, i * 8:(i + 1) * 8], in0=ppx[:, i * 8:(i + 1) * 8],
                                                       in1=bada_pp[:, i * 8:(i + 1) * 8], op=ALU.add),
                 reads=[t_mods, t_const], writes=[t_mods])

        def finish_mods(gs, sh, nw, i_sh, i_sc):
            S.op("dve", lambda e: e.scalar_tensor_tensor(
                out=gs, in0=ppx[:, i_sc * 8:(i_sc + 1) * 8], scalar=1.0, in1=nw, op0=ALU.add, op1=ALU.mult),
                reads=[t_mods, t_const], writes=[t_mods])
            S.op("dve", lambda e: e.tensor_scalar(out=gs, in0=gs, scalar1=32.0, scalar2=None, op0=ALU.mult),
                 reads=[t_mods], writes=[t_mods])
            S.op("dve", lambda e: e.tensor_copy(out=sh, in_=ppx[:, i_sh * 8:(i_sh + 1) * 8]),
                 reads=[t_mods], writes=[t_mods])

        finish_mods(gs1, sh1, n1w, 0, 1)
        S.barrier()

        def prep_A(src_ap, xin, t_xin, xs, t_xs, junk, t_junk, st, t_st, dma_src=True, src_tiles=(),
                   scale_eng="act"):
            if dma_src:
                dma("sp", xin, src_ap, writes=[t_xin])
                rd = [t_xin]
            else:
                xin = src_ap
                rd = list(src_tiles)
            S.op("act", lambda e: e.activation(out=junk, in_=xin, func=AF.Square, accum_out=st[:, 0:1]),
                 reads=rd, writes=[t_junk, t_st])
            S.op("act", lambda e: e.activation(out=st[:, 1:2], in_=st[:, 0:1], func=AF.Ln, scale=1.0,
                                               bias=st[:, 3:4]),
                 reads=[t_st], writes=[t_st])
            S.op("act", lambda e: e.activation(out=st[:, 2:3], in_=st[:, 1:2], func=AF.Exp, scale=-0.5),
                 reads=[t_st], writes=[t_st])
            if scale_eng == "act":
                S.op("act", lambda e: e.activation(out=xs, in_=xin, func=AF.Copy, scale=st[:, 2:3]),
                     reads=rd + [t_st], writes=[t_xs])
            else:
                S.op("pool", lambda e: e.tensor_scalar(out=xs, in0=xin, scalar1=st[:, 2:3], scalar2=None,
                                                       op0=ALU.mult),
                     reads=rd + [t_st], writes=[t_xs])

        def prep_B(xs, t_xs, tpk, tmpT, t_tmpT, dst, t_dst, gs, sh, extra_deps=()):
            tp = bank_bf(tpk).rearrange("p (a b) -> p a b", a=8)

            def tr(e):
                ins = None
                for kc in range(8):
                    ins = e.transpose(out=tp[:, kc, :], in_=xs[:, kc * 128:(kc + 1) * 128], identity=ident)
                return ins
            S.op("pe", tr, reads=[t_xs, t_const], writes=[t_bank[tpk]])
            S.op("dve", lambda e: e.tensor_tensor(out=tmpT, in0=tp, in1=bc(gs.unsqueeze(2), [128, 8, 128]),
                                                  op=ALU.mult),
                 reads=[t_bank[tpk], t_mods], writes=[t_tmpT])
            S.op("dve", lambda e: e.tensor_tensor(out=dst, in0=tmpT, in1=bc(sh.unsqueeze(2), [128, 8, 128]),
                                                  op=ALU.add),
                 reads=[t_tmpT, t_mods], writes=[t_dst], extra_deps=extra_deps)

        NSTAT = 8
        t_stat = [Tile(f"stat{i}") for i in range(NSTAT)]
        for i in range(NSTAT):
            S.op("dve", lambda e, i=i: e.memset(stat[:, i * 8 + 3:i * 8 + 4], 1024.0 * EPS), writes=[t_stat[i]])
        eps64 = stat[:, 200:201]
        S.op("dve", lambda e: e.memset(eps64, 64.0 * EPS), writes=[t_const])

        qT = V(46 * KB, [128, 4, 2048], BF16)
        kT = V(62 * KB, [128, 4, 2304], BF16)
        Vaug = V(80 * KB, [128, 18, 8, 65], BF16)
        hT_own = V(99 * KB, [128, 8, 2048], BF16)
        uT = V(131 * KB, [128, 4, 4096], BF16)
        xin = [V(163 * KB, [128, 1024], F32), V(167 * KB, [128, 1024], F32)]
        xs = [V(171 * KB + i * 2 * KB, [128, 1024], BF16) for i in range(4)]
        junk = V(179 * KB, [128, 1024], BF16)
        tmpT = V(181 * KB, [128, 8, 128], F32)
        hTs_x = [V(185 * KB, [128, 8, 512], BF16)]
        zq = [V(193 * KB + i * 2 * KB, [128, 512], F32) for i in range(3)]
        sq = [V(199 * KB + i * KB, [128, 512], BF16) for i in range(3)]
        rstd = [V(202 * KB + i * 2 * KB, [128, 512], F32) for i in range(2)]
        t_qT = [Tile() for _ in range(4)]
        t_kT = [Tile() for _ in range(5)]
        t_Vaug = [Tile() for _ in range(18)]
        t_hTo = [Tile() for _ in range(4)]
        t_uT = [Tile() for _ in range(8)]
        t_xin = [Tile(), Tile()]
        t_xs = [Tile() for _ in range(4)]
        t_junk, t_tmpT = Tile(), Tile()
        t_hTsx = [Tile()]
        t_zq, t_sq, t_rstd = [Tile() for _ in range(3)], [Tile() for _ in range(3)], [Tile(), Tile()]


        mm_banks = [2, 3, 4, 5]
        mm_ctr = [0]

        def next_bank():
            k = mm_banks[mm_ctr[0] % len(mm_banks)]
            mm_ctr[0] += 1
            return k

        def hts_of(Sx):
            if Sx < 4:
                return hT_own[:, :, Sx * 512:(Sx + 1) * 512], t_hTo[Sx]
            return hTs_x[0], t_hTsx[0]

        def emit_A_tile(Sx, tl, slot=None):
            slot = tl if slot is None else slot
            g = 4 * Sx + tl
            prep_A(x_d[g * 128:(g + 1) * 128, :], xin[g % 2], t_xin[g % 2], xs[slot], t_xs[slot], junk, t_junk,
                   stat[:, (g % NSTAT) * 8:(g % NSTAT) * 8 + 8], t_stat[g % NSTAT])

        def emit_B_tile(Sx, tl, slot=None):
            slot = tl if slot is None else slot
            hts, t_h = hts_of(Sx)
            g = 4 * Sx + tl
            prep_B(xs[slot], t_xs[slot], g % 2, tmpT, t_tmpT, hts[:, :, tl * 128:(tl + 1) * 128], t_h, gs1, sh1)

        def proj_fm(col0, hts, t_h, n=512, t_w=None):
            t_w = [t_w] if t_w is not None else t_W
            k = next_bank()

            def mm(e):
                ins = None
                for kc in range(8):
                    ins = e.matmul(bank(k)[:, 0:n], lhsT=Wqkvu[:, kc, col0:col0 + 128],
                                   rhs=hts[:, kc, 0:n], start=(kc == 0), stop=(kc == 7))
                return ins
            S.op("pe", mm, reads=list(t_w) + [t_h], writes=[t_bank[k]])
            return k

        qk_ctr = [0]

        def qk_stage1(k, n, bias_col):
            zb = qk_ctr[0] % 3
            qk_ctr[0] += 1
            S.op("dve", lambda e: e.tensor_scalar(out=zq[zb][:, 0:n], in0=bank(k)[:, 0:n],
                                                  scalar1=bin_pp[:, bias_col:bias_col + 1], scalar2=None,
                                                  op0=ALU.add),
                 reads=[t_bank[k], t_const], writes=[t_zq[zb]])
            S.op("act", lambda e: e.activation(out=sq[zb][:, 0:n], in_=zq[zb][:, 0:n], func=AF.Square),
                 reads=[t_zq[zb]], writes=[t_sq[zb]])
            return zb

        def qk_stage2(zb, n, wcol, dst, t_dst):
            k2 = 6 + (zb % 2)
            rb = zb % 2
            S.op("pe", lambda e: e.matmul(bank(k2)[:, 0:n], lhsT=bones, rhs=sq[zb][:, 0:n], start=True, stop=True),
                 reads=[t_sq[zb], t_const], writes=[t_bank[k2]])
            S.op("act", lambda e: e.activation(out=rstd[rb][:, 0:n], in_=bank(k2)[:, 0:n], func=AF.Ln, scale=1.0,
                                               bias=eps64),
                 reads=[t_bank[k2], t_const], writes=[t_rstd[rb]])
            S.op("act", lambda e: e.activation(out=rstd[rb][:, 0:n], in_=rstd[rb][:, 0:n], func=AF.Exp, scale=-0.5),
                 reads=[t_rstd[rb]], writes=[t_rstd[rb]])
            S.op("dve", lambda e: e.scalar_tensor_tensor(out=dst, in0=zq[zb][:, 0:n], scalar=wcol,
                                                         in1=rstd[rb][:, 0:n], op0=ALU.mult, op1=ALU.mult),
                 reads=[t_zq[zb], t_rstd[rb], t_const], writes=[t_dst])

        def emit_group(cur, nxt):
            items = []
            for Sx in cur:
                for idx in range(4):
                    items.append((Sx, "u", idx))
                    if Sx < 4:
                        items.append((Sx, "q", idx))
                    if Sx < 5:
                        items.append((Sx, "k", idx))
                        if Sx < 4 or idx < 2:
                            items.append((Sx, "v", idx))
            ptiles = []
            if nxt:
                for Sn in sorted(nxt, key=lambda v: v >= 4):
                    ptiles += [(Sn, tl) for tl in range(4)]
            NT = max(len(ptiles), 1)
            n = len(items)
            LAG = 2
            pend = {}
            npf = 0
            nbf = 0
            for i in range(n + LAG):
                if npf < len(ptiles) and i >= (npf * max(n - 4, 0)) // NT:
                    emit_A_tile(ptiles[npf][0], ptiles[npf][1], npf % 4)
                    npf += 1
                if nbf < npf and i >= (nbf * max(n - 4, 0)) // NT + 3:
                    emit_B_tile(ptiles[nbf][0], ptiles[nbf][1], nbf % 4)
                    nbf += 1
                if i < n:
                    Sx, kind, idx = items[i]
                    hts, t_h = hts_of(Sx)
                    nk = 512 if Sx < 4 else 256
                    if kind == "u":
                        k = proj_fm(1536 + idx * 128, hts, t_h, t_w=t_Wu)
                        S.op("dve", lambda e, k=k, gq=idx, Sx=Sx: e.tensor_scalar(
                            out=uT[:, gq, Sx * 512:(Sx + 1) * 512], in0=bank(k), scalar1=bin_pp[:, 12 + gq:13 + gq],
                            scalar2=None, op0=ALU.add),
                            reads=[t_bank[k], t_const], writes=[t_uT[Sx]])
                    elif kind == "q":
                        k = proj_fm(idx * 128, hts, t_h)
                        pend[i] = (qk_stage1(k, 512, idx), 512, qkw[:, 0:1],
                                   qT[:, idx, Sx * 512:(Sx + 1) * 512], t_qT[Sx])
                    elif kind == "k":
                        k = proj_fm(512 + idx * 128, hts, t_h, nk)
                        pend[i] = (qk_stage1(k, nk, 4 + idx), nk, kw8[:, 0:1],
                                   kT[:, idx, Sx * 512:Sx * 512 + nk], t_kT[Sx])
                    else:
                        g = 4 * Sx + idx
                        k = next_bank()

                        def mmv(e, k=k, tl=idx, hts=hts):
                            ins = None
                            for kc in range(8):
                                ins = e.matmul(bank(k), lhsT=hts[:, kc, tl * 128:(tl + 1) * 128],
                                               rhs=Wqkvu[:, kc, 1024:1536], start=(kc == 0), stop=(kc == 7))
                            return ins
                        S.op("pe", mmv, reads=t_W + [t_h], writes=[t_bank[k]])
                        S.op("dve", lambda e, k=k, g=g: e.tensor_copy(
                            out=Vaug[:, g, :, 0:64], in_=bank(k).rearrange("p (a b) -> p a b", a=8)),
                            reads=[t_bank[k]], writes=[t_Vaug[g]])
                j = i - LAG
                if j in pend:
                    qk_stage2(*pend.pop(j))
            while nbf < len(ptiles):
                if npf <= nbf:
                    emit_A_tile(ptiles[npf][0], ptiles[npf][1], npf % 4)
                    npf += 1
                emit_B_tile(ptiles[nbf][0], ptiles[nbf][1], nbf % 4)
                nbf += 1

        groups = [[5, 0], [6, 1], [7, 2], [4, 3]]
        for Sn in groups[0]:
            for tl in range(4):
                emit_A_tile(Sn, tl)
            for tl in range(4):
                emit_B_tile(Sn, tl)
        cast_ops = []
        for kc in range(8):
            if kc % 2 == 0:
                cast_ops.append(S.op("act", lambda e, kc=kc: e.activation(out=Wqkvu[:, kc, 0:1536], in_=stg_w[kc],
                                                                          func=AF.Copy),
                                     reads=[t_stgw[kc]], writes=[t_W[0]]))
            else:
                cast_ops.append(S.op("dve", lambda e, kc=kc: e.tensor_copy(out=Wqkvu[:, kc, 0:1536], in_=stg_w[kc]),
                                     reads=[t_stgw[kc]], writes=[t_W[1]]))
        S.op("dve", lambda e: e.memset(Vaug[:, :, :, 64:65], 1.0), writes=t_Vaug, extra_deps=cast_ops)
        for gi, cur in enumerate(groups):
            emit_group(cur, groups[gi + 1] if gi + 1 < len(groups) else None)
        S.barrier()

        btab = V(14 * KB, [128, 8, 3, 640], BF16)
        attT = V(163 * KB, [128, 4, 2048], BF16)
        NSB = 3
        tmpS = [V(179 * KB + i * 2560, [128, 640], BF16) for i in range(NSB)]
        Pb = [V(187 * KB + i * 1280, [128, 640], BF16) for i in range(NSB)]
        att = [V(191 * KB, [128, 8, 64], BF16), V(192 * KB, [128, 8, 64], BF16)]
        rec = [V(193 * KB, [128, 8], F32), V(193 * KB + 32, [128, 8], F32)]
        t_attT = [Tile() for _ in range(16)]
        t_tmpS = [Tile() for _ in range(NSB)]
        t_Pb = [Tile() for _ in range(NSB)]
        t_att = [Tile(), Tile()]
        t_rec = [Tile(), Tile()]
        t_S = [Tile(), Tile()]
        t_O = [Tile(), Tile()]
        t_tpa = Tile()
        S_ps = [PS[0], PS[1]]
        btab_v = btab.rearrange("p a b c -> p (a b c)")
        t_btab = [[Tile() for _ in range(4)] for _ in range(3)]
        btab_d4 = btab_d.rearrange("p (a b c) -> p a b c", a=8, b=3)
        def btab_load(kd, hq):
            hs = slice(2 * hq, 2 * hq + 2)
            dma("pool", btab[:, hs, kd, :], btab_d4[:, hs, kd, :], writes=[t_btab[kd][hq]])

        def btab_exp(kd, hq):
            hs = slice(2 * hq, 2 * hq + 2)
            S.op("act", lambda e: e.activation(out=btab[:, hs, kd, :], in_=btab[:, hs, kd, :], func=AF.Exp),
                 reads=[t_btab[kd][hq]], writes=[t_btab[kd][hq]])

        for kd in range(3):
            for hq in range(4):
                btab_load(kd, hq)
        for hq in range(4):
            btab_exp(0, hq)
        Ops = [PS[2][:, 0:260].rearrange("p (a b) -> p a b", a=4),
               PS[2][:, 512:772].rearrange("p (a b) -> p a b", a=4)]
        tpa = bank_bf(6)[:, 0:512].rearrange("p (a b) -> p a b", a=4)

        def att_S(lp, h, it):
            c0 = max(lp - 2, 0)
            hp, h2 = h // 2, h % 2
            pr = slice(64 * h2, 64 * h2 + 64)
            sbi = it % NSB
            spi = it % 2
            Sps = S_ps[spi]

            def mm(e):
                ins = None
                for jj in range(5):
                    ins = e.matmul(Sps[:, jj * 128:(jj + 1) * 128],
                                   lhsT=kT[pr, hp, (c0 + jj) * 128:(c0 + jj + 1) * 128],
                                   rhs=qT[pr, hp, lp * 128:(lp + 1) * 128], start=True, stop=True)
                return ins
            S.op("pe", mm, reads=t_kT + t_qT, writes=[t_S[spi]])
            kind = min(lp, 2)
            S.op("act", lambda e: e.activation(out=tmpS[sbi], in_=Sps[:, 0:640], func=AF.Exp),
                 reads=[t_S[spi]], writes=[t_tmpS[sbi]])
            S.op("dve", lambda e: e.tensor_tensor(out=Pb[sbi], in0=tmpS[sbi], in1=btab[:, h, kind, :],
                                                  op=ALU.mult),
                 reads=[t_tmpS[sbi], t_btab[kind][h // 2]], writes=[t_Pb[sbi]])

        def att_PV(lp, h, it):
            c0 = max(lp - 2, 0)
            sbi = it % NSB

            def mm(e):
                ins = None
                for jj in range(5):
                    ins = e.matmul(Ops[h // 4][:, h % 4, :], lhsT=Pb[sbi][:, jj * 128:(jj + 1) * 128],
                                   rhs=Vaug[:, c0 + jj, h, :], start=(jj == 0), stop=(jj == 4))
                return ins
            S.op("pe", mm, reads=[t_Pb[sbi]] + t_Vaug, writes=[t_O[h // 4]])

        def att_fin(lp, halves=(0, 1)):
            b = lp % 2
            for half in halves:
                S.op("dve", lambda e, half=half: e.reciprocal(out=rec[b][:, half * 4:half * 4 + 4],
                                                              in_=Ops[half][:, :, 64]),
                     reads=[t_O[half]], writes=[t_rec[b]])
                S.op("dve", lambda e, half=half: e.tensor_tensor(
                    out=att[b][:, half * 4:half * 4 + 4, :], in0=Ops[half][:, :, 0:64],
                    in1=bc(rec[b][:, half * 4:half * 4 + 4].unsqueeze(2), [128, 4, 64]), op=ALU.mult),
                    reads=[t_O[half], t_rec[b]], writes=[t_att[b]])

        def att_fin_B(lp):
            b = lp % 2
            attf = att[b].rearrange("p a b -> p (a b)")

            def tr(e):
                ins = None
                for hp in range(4):
                    ins = e.transpose(out=tpa[:, hp, :], in_=attf[:, hp * 128:(hp + 1) * 128], identity=ident)
                return ins
            S.op("pe", tr, reads=[t_att[b], t_const], writes=[t_tpa])
            S.op("dve", lambda e: e.tensor_tensor(out=attT[:, :, lp * 128:(lp + 1) * 128], in0=tpa,
                                                  in1=bc(bin_pp[:, 8:12].unsqueeze(2), [128, 4, 128]), op=ALU.add),
                 reads=[t_tpa, t_const], writes=[t_attT[lp]])

        screp2 = V(44 * KB, [128, 8, 128], BF16)
        wada2 = V(194 * KB, [128, 8, 512], BF16)
        modblk = V(202 * KB, [128, 512], F32)
        tmpd2 = V(204 * KB, [128, 4, 128], F32)
        t_screp2, t_wada2, t_modblk, t_tmpd2, t_b7 = Tile(), Tile(), Tile(), Tile(), Tile()
        S.op("dve", lambda e: e.tensor_copy(out=screp2, in_=bc(sc_sb.unsqueeze(2), [128, 8, 128])),
             reads=[t_const], writes=[t_screp2])

        def p0b_load(nb):
            dma("pool", wada2, wada_v[:, :, nb * 512:(nb + 1) * 512], writes=[t_wada2])

        def p0b_block(nb):
            sec, half = nb // 2, nb % 2

            def mm(e):
                ins = None
                for kc in range(8):
                    ins = e.matmul(bank(7), lhsT=screp2[:, kc, :], rhs=wada2[:, kc, :], start=(kc == 0),
                                   stop=(kc == 7))
                return ins
            S.op("pe", mm, reads=[t_screp2, t_wada2], writes=[t_b7])
            if sec in (2, 5):
                gbc = gate1_bc if sec == 2 else gate2_bc
                dst = gbc[:, half * 512:(half + 1) * 512]
                S.op("dve", lambda e: e.tensor_tensor(out=dst, in0=bank(7), in1=dst, op=ALU.add),
                     reads=[t_b7, t_gate], writes=[t_gate])
            else:
                i = sec - 1
                S.op("act", lambda e: e.activation(out=modblk, in_=bank(7), func=AF.Copy),
                     reads=[t_b7], writes=[t_modblk])
                S.op("dve", lambda e: e.tensor_tensor(out=tmpd2, in0=modblk.rearrange("p (a b) -> p a b", a=4),
                                                      in1=bc(identf.unsqueeze(1), [128, 4, 128]), op=ALU.mult),
                     reads=[t_modblk, t_const], writes=[t_tmpd2])
                c0_ = i * 8 + half * 4
                S.op("dve", lambda e: e.tensor_reduce(out=ppx[:, c0_:c0_ + 4], in_=tmpd2, axis=AX.X, op=ALU.add),
                     reads=[t_tmpd2], writes=[t_mods])
                S.op("dve", lambda e: e.tensor_tensor(out=ppx[:, c0_:c0_ + 4], in0=ppx[:, c0_:c0_ + 4],
                                                      in1=bada_pp[:, sec * 8 + half * 4:sec * 8 + half * 4 + 4],
                                                      op=ALU.add),
                     reads=[t_mods, t_const], writes=[t_mods])

        items = [(lp, h) for lp in range(16) for h in range(8)]
        AHEAD = 2
        for j in range(AHEAD):
            att_S(items[j][0], items[j][1], j)
        for it, (lp, h) in enumerate(items):
            if it + AHEAD < len(items):
                att_S(items[it + AHEAD][0], items[it + AHEAD][1], it + AHEAD)
            att_PV(lp, h, it)
            if 2 <= it < 6:
                btab_exp(1, it - 2)
            if 9 <= it < 13:
                btab_exp(2, it - 9)
            if h == 3:
                att_fin(lp, (0,))
            if h == 7:
                att_fin(lp, (1,))
            if h == 3 and lp > 0:
                att_fin_B(lp - 1)
            if it % 14 == 0 and it // 14 < 8:
                p0b_load(4 + it // 14)
            if it % 14 == 10 and it // 14 < 8:
                p0b_block(4 + it // 14)
        att_fin_B(15)
        finish_mods(gs2, sh2, n2w, 2, 3)
        S.barrier()

        Vt = V(14 * KB, [128, 32, 1024], BF16)
        YT = V(78 * KB, [128, 4, 2048], BF16)
        dtab = [V(179 * KB, [128, 2, 4, 512], BF16), V(187 * KB, [128, 2, 4, 512], BF16)]
        t_Vt = [Tile() for _ in range(32)]
        t_YT = [Tile() for _ in range(4)]
        t_dtab = [Tile(), Tile()]
        t_PS = [Tile() for _ in range(4)]
        for g in range(32):
            vp = PS[g % 2]

            def mm(e, g=g, vp=vp):
                ins = None
                for gq in range(4):
                    ins = e.matmul(vp[:, gq * 256:(gq + 1) * 256], lhsT=uT[:, gq, g * 128:(g + 1) * 128], rhs=cs_t,
                                   start=True, stop=True)
                return ins
            last_v = S.op("pe", mm, reads=t_uT + [t_const], writes=[t_PS[g % 2]])
            if g % 2 == 0:
                S.op("act", lambda e, g=g, vp=vp: e.activation(out=Vt[:, g, :], in_=vp, func=AF.Copy),
                     reads=[t_PS[g % 2]], writes=[t_Vt[g]])
            else:
                S.op("dve", lambda e, g=g, vp=vp: e.tensor_copy(out=Vt[:, g, :], in_=vp),
                     reads=[t_PS[g % 2]], writes=[t_Vt[g]])
        Wg = V(131 * KB, [128, 8, 2048], BF16)
        Wna = V(195 * KB, [128, 4, 1024], BF16)
        t_Wg, t_Wna, t_Wfn = Tile(), Tile(), Tile()
        for kc in range(8):
            dma("pool", Wg[:, kc, :], win_v[:, kc, 2048:4096], writes=[t_Wg], extra_deps=[last_v])
        dma("pool", Wna, wna_d.rearrange("(a p) n -> p a n", p=128), writes=[t_Wna])
        t_acc = [Tile() for _ in range(4)]
        Wfn_pre = V(46 * KB, [128, 4, 1024], BF16)
        for sb in range(4):
            for sl in range(8):
                blk = sb * 8 + sl
                db = blk % 2
                dma("sp", dtab[db].rearrange("p a b c -> p (a b c)"), dft_d[blk], writes=[t_dtab[db]])

                def mm(e, sl=sl, db=db):
                    ins = None
                    for ti in range(4):
                        g = sl * 4 + ti
                        for gq in range(4):
                            e.matmul(bank(4 + gq), lhsT=Vt[:, g, gq * 256:gq * 256 + 128], rhs=dtab[db][:, 0, ti, :],
                                     start=(g == 0), stop=False)
                            ins = e.matmul(bank(4 + gq), lhsT=Vt[:, g, gq * 256 + 128:gq * 256 + 256],
                                           rhs=dtab[db][:, 1, ti, :], start=False, stop=(g == 31))
                    return ins
                pos_op = S.op("pe", mm, reads=[t_dtab[db]] + t_Vt[sl * 4:sl * 4 + 4], writes=t_acc)
                if sb == 3 and sl == 4:
                    dma("pool", Wfn_pre, wfn_d.rearrange("(a p) n -> p a n", p=128), writes=[t_Wfn],
                        extra_deps=[pos_op])
            for gq in range(4):
                if gq % 2 == 0:
                    S.op("act", lambda e, gq=gq, sb=sb: e.activation(out=YT[:, gq, sb * 512:(sb + 1) * 512],
                                                                     in_=bank(4 + gq), func=AF.Copy),
                         reads=t_acc, writes=[t_YT[sb]])
                else:
                    S.op("dve", lambda e, gq=gq, sb=sb: e.tensor_copy(out=YT[:, gq, sb * 512:(sb + 1) * 512],
                                                                      in_=bank(4 + gq)),
                         reads=t_acc, writes=[t_YT[sb]])
        S.barrier()

        mT = V(14 * KB, [128, 8, 2048], BF16)
        Wfn = V(46 * KB, [128, 4, 1024], BF16)
        sa = [V(54 * KB, [128, 512], F32), V(56 * KB, [128, 512], F32)]
        sbg = [V(58 * KB, [128, 512], F32), V(60 * KB, [128, 512], F32)]
        t1 = [V(62 * KB, [128, 512], F32), V(64 * KB, [128, 512], F32)]
        t2 = [V(66 * KB, [128, 512], F32), V(68 * KB, [128, 512], F32)]
        stgo = [V(70 * KB, [128, 1024], F32), V(74 * KB, [128, 1024], F32)]
        Wo = V(179 * KB, [128, 8, 1024], BF16)
        t_sa, t_sbg, t_t1, t_t2 = [Tile(), Tile()], [Tile(), Tile()], [Tile(), Tile()], [Tile(), Tile()]
        t_mT = [[Tile() for _ in range(4)] for _ in range(8)]
        t_Wo, t_stgo = Tile(), [Tile(), Tile()]
        wo_v = wo_d.rearrange("(kc p) n -> p kc n", p=128)
        it = 0
        n_wo = 0
        for oc in range(8):
            for tb in range(4):
                cols = slice(tb * 512, (tb + 1) * 512)
                st = it % 2
                kb = 4 * st
                it += 1

                def mm(e, oc=oc, cols=cols, kb=kb):
                    for kc in range(8):
                        e.matmul(bank(kb), lhsT=Wg[:, kc, oc * 128:(oc + 1) * 128], rhs=hT_own[:, kc, cols],
                                 start=(kc == 0), stop=(kc == 7))
                    for kc in range(8):
                        e.matmul(bank(kb + 1), lhsT=Wg[:, kc, 1024 + oc * 128:1024 + (oc + 1) * 128],
                                 rhs=hT_own[:, kc, cols], start=(kc == 0), stop=(kc == 7))
                    for hp in range(4):
                        e.matmul(bank(kb + 2), lhsT=Wna[:, hp, oc * 128:(oc + 1) * 128], rhs=attT[:, hp, cols],
                                 start=(hp == 0), stop=(hp == 3))
                    ins = None
                    for gq in range(4):
                        ins = e.matmul(bank(kb + 3), lhsT=Wfn[:, gq, oc * 128:(oc + 1) * 128], rhs=YT[:, gq, cols],
                                       start=(gq == 0), stop=(gq == 3))
                    return ins
                S.op("pe", mm, reads=[t_Wg, t_Wna, t_Wfn] + t_hTo + t_attT + t_YT,
                     writes=[t_PS[2 * st], t_PS[2 * st + 1]])
                S.op("act", lambda e, oc=oc, kb=kb, st=st: e.activation(out=sa[st], in_=bank(kb), func=AF.Sigmoid,
                                                                        bias=bin_pp[:, 16 + oc:17 + oc], scale=1.0),
                     reads=[t_PS[2 * st], t_const], writes=[t_sa[st]])
                S.op("act", lambda e, oc=oc, kb=kb, st=st: e.activation(out=sbg[st], in_=bank(kb + 1), func=AF.Sigmoid,
                                                                        bias=bin_pp[:, 24 + oc:25 + oc], scale=1.0),
                     reads=[t_PS[2 * st], t_const], writes=[t_sbg[st]])
                S.op("dve", lambda e, kb=kb, st=st: e.tensor_tensor(out=t1[st], in0=bank(kb + 2), in1=sa[st],
                                                                    op=ALU.mult),
                     reads=[t_PS[2 * st + 1], t_sa[st]], writes=[t_t1[st]])
                S.op("dve", lambda e, oc=oc, kb=kb, st=st: e.scalar_tensor_tensor(
                    out=t2[st], in0=bank(kb + 3), scalar=bfn_pp[:, oc:oc + 1], in1=sbg[st], op0=ALU.add,
                    op1=ALU.mult),
                    reads=[t_PS[2 * st + 1], t_sbg[st], t_const], writes=[t_t2[st]])
                S.op("pool", lambda e, oc=oc, cols=cols, st=st: e.tensor_tensor(out=mT[:, oc, cols], in0=t1[st],
                                                                                in1=t2[st], op=ALU.add),
                     reads=[t_t1[st], t_t2[st]], writes=[t_mT[oc][tb]])
                if it >= 3 and it % 3 == 0 and n_wo < 8:
                    i = n_wo
                    n_wo += 1
                    dma("sp", stgo[i % 2], wo_v[:, i, :], writes=[t_stgo[i % 2]])
                    S.op("pool", lambda e, i=i: e.tensor_tensor(out=Wo[:, i, :], in0=stgo[i % 2], in1=gate1_bc,
                                                                op=ALU.mult),
                         reads=[t_stgo[i % 2], t_gate], writes=[t_Wo])
        assert n_wo == 8
        S.barrier()
        xr = [V(163 * KB, [128, 1024], F32), V(167 * KB, [128, 1024], F32)]
        acc = V(99 * KB, [128, 16, 1024], F32)
        t_xr = [Tile(), Tile()]
        t_accg = [Tile() for _ in range(16)]
        W1q = [V(46 * KB, [128, 8, 1024], BF16), V(62 * KB, [128, 8, 1024], BF16)]
        W2q = [V(78 * KB, [128, 8, 1024], BF16), V(163 * KB, [128, 8, 1024], BF16)]
        stg2 = [V(195 * KB, [128, 1024], F32), V(199 * KB, [128, 1024], F32)]
        t_W1q, t_W2q = [Tile(), Tile()], [Tile(), Tile()]
        t_stg2 = [Tile(), Tile()]
        w1_v = w1_d.rearrange("(kc p) n -> p kc n", p=128)
        w2_v = w2_d.rearrange("(fc p) n -> p fc n", p=128)

        def load_w1(qt):
            b = qt % 2
            for kc in range(8):
                dma("pool", W1q[b][:, kc, :], w1_v[:, kc, qt * 1024:(qt + 1) * 1024], writes=[t_W1q[b]])

        def load_w2_piece(qt, fc):
            b = qt % 2
            i = qt * 8 + fc
            dma("sp", stg2[i % 2], w2_v[:, i, :], writes=[t_stg2[i % 2]])
            S.op("pool", lambda e: e.tensor_tensor(out=W2q[b][:, fc, :], in0=stg2[i % 2], in1=gate2_bc, op=ALU.mult),
                 reads=[t_stg2[i % 2], t_gate], writes=[t_W2q[b]])

        def load_quarter(qt):
            load_w1(qt)
            for fc in range(8):
                load_w2_piece(qt, fc)

        h2T = V(14 * KB, [128, 8, 2048], BF16)
        xs2 = [V(62 * KB + i * 2 * KB, [128, 1024], BF16) for i in range(3)]
        junk2 = V(68 * KB, [128, 1024], BF16)
        tmpT2 = V(70 * KB, [128, 8, 128], F32)
        t_h2T = [Tile() for _ in range(16)]
        t_xs2 = [Tile(), Tile(), Tile()]
        t_junk2, t_tmpT2 = Tile(), Tile()
        mm_ops = {}

        def emit_h2T(g):
            prep_B(xs2[g % 3], t_xs2[g % 3], 4 + g % 2, tmpT2, t_tmpT2, h2T[:, :, g * 128:(g + 1) * 128], t_h2T[g],
                   gs2, sh2, extra_deps=[mm_ops[g]])

        load_w1(0)
        for g in range(16):
            dma("sp", xr[g % 2], x_d[g * 128:(g + 1) * 128, :], writes=[t_xr[g % 2]])
            Cp = PS[g % 2]

            def mm(e, g=g, Cp=Cp):
                ins = None
                for half in range(2):
                    for oc in range(8):
                        ins = e.matmul(Cp[:, half * 512:(half + 1) * 512], lhsT=mT[:, oc, g * 128:(g + 1) * 128],
                                       rhs=Wo[:, oc, half * 512:(half + 1) * 512], start=(oc == 0), stop=(oc == 7))
                return ins
            mm_op = S.op("pe", mm, reads=[t_Wo] + [t_mT[oc][g // 4] for oc in range(8)], writes=[t_PS[g % 2]])
            S.op("dve", lambda e, g=g, Cp=Cp: e.tensor_tensor(out=acc[:, g, :], in0=Cp, in1=xr[g % 2], op=ALU.add),
                 reads=[t_PS[g % 2], t_xr[g % 2]], writes=[t_accg[g]])
            mm_ops[g] = mm_op
            prep_A(acc[:, g, :], None, None, xs2[g % 3], t_xs2[g % 3], junk2, t_junk2,
                   stat[:, (g % NSTAT) * 8:(g % NSTAT) * 8 + 8], t_stat[g % NSTAT], dma_src=False,
                   src_tiles=[t_accg[g]])
            if g >= 2:
                emit_h2T(g - 2)
            if g >= 8:
                load_w2_piece(0, g - 8)
        emit_h2T(14)
        emit_h2T(15)
        S.barrier()

        aTq = [V(179 * KB, [128, 8, 512], BF16), V(187 * KB, [128, 8, 512], BF16)]
        sqf = [V(94 * KB, [128, 512], F32), V(96 * KB, [128, 512], F32)]
        t_aTq = [Tile(), Tile()]
        t_sqf = [Tile(), Tile()]
        out_ops = []
        fit = [0]

        def mlp_up(qt, tb):
            b = qt % 2
            ab = (qt * 4 + tb) % 2
            for fc in range(8):
                kf = fit[0] % 2
                fit[0] += 1

                def mm(e, fc=fc, kf=kf):
                    ins = None
                    for kc in range(8):
                        ins = e.matmul(bank(kf), lhsT=W1q[b][:, kc, fc * 128:(fc + 1) * 128],
                                       rhs=h2T[:, kc, tb * 512:(tb + 1) * 512], start=(kc == 0), stop=(kc == 7))
                    return ins
                S.op("pe", mm, reads=[t_W1q[b]] + t_h2T[tb * 4:tb * 4 + 4], writes=[t_bank[kf]])
                S.op("act", lambda e, kf=kf: e.activation(out=sqf[kf], in_=bank(kf), func=AF.Square),
                     reads=[t_bank[kf]], writes=[t_sqf[kf]])
                S.op("dve", lambda e, kf=kf, fc=fc: e.scalar_tensor_tensor(
                    out=aTq[ab][:, fc, :], in0=bank(kf), scalar=0.0, in1=sqf[kf], op0=ALU.is_gt, op1=ALU.mult),
                    reads=[t_bank[kf], t_sqf[kf]], writes=[t_aTq[ab]])

        def mlp_down(qt, tb):
            b = qt % 2
            ab = (qt * 4 + tb) % 2
            for tl in range(4):
                g = tb * 4 + tl
                Dp = PS[2 + g % 2]

                def mm2(e, tl=tl, Dp=Dp):
                    ins = None
                    for half in range(2):
                        for fc in range(8):
                            ins = e.matmul(Dp[:, half * 512:(half + 1) * 512],
                                           lhsT=aTq[ab][:, fc, tl * 128:(tl + 1) * 128],
                                           rhs=W2q[b][:, fc, half * 512:(half + 1) * 512], start=(fc == 0),
                                           stop=(fc == 7))
                    return ins
                S.op("pe", mm2, reads=[t_aTq[ab], t_W2q[b]], writes=[t_PS[2 + g % 2]])
                S.op("dve", lambda e, g=g, Dp=Dp: e.tensor_tensor(out=acc[:, g, :], in0=Dp, in1=acc[:, g, :],
                                                                  op=ALU.add),
                     reads=[t_PS[2 + g % 2], t_accg[g]], writes=[t_accg[g]])
                if qt == 3:
                    out_ops.append(dma("sp", out_d[g * 128:(g + 1) * 128, :], acc[:, g, :], reads=[t_accg[g]]))

        steps = [(qt, tb) for qt in range(4) for tb in range(4)]
        load_quarter(1)
        mlp_up(*steps[0])
        for i, (qt, tb) in enumerate(steps):
            if i + 1 < len(steps):
                nq, ntb = steps[i + 1]
                if ntb == 0 and False:
                    pass
                mlp_up(nq, ntb)
            mlp_down(qt, tb)
            if tb == 3 and qt + 2 < 4:
                load_quarter(qt + 2)
        S.barrier()

        S.finalize(sems)
        with nc.Block() as block:
            @block.tensor
            def _(e):
                S.replay("pe", e)

            @block.scalar
            def _(e):
                S.replay("act", e)

            @block.vector
            def _(e):
                S.replay("dve", e)

            @block.gpsimd
            def _(e):
                S.replay("pool", e)

            @block.sync
            def _(e):
                S.replay("sp", e)
    return nc


def _rows(hf):
    return np.arange(64) if hf == 0 else np.arange(63, -1, -1)


def _bias_tables(rpb, hf):
    R = _rows(hf)
    tab = np.full((128, 8, 3, 640), NEG, np.float32)
    qcol = np.arange(64)
    kcol = np.arange(64)
    cs = np.clip(qcol - 8, 0, 48)
    colok = (kcol[:, None] >= cs[None, :]) & (kcol[:, None] <= cs[None, :] + 15)
    cidx = np.clip(kcol[:, None] - qcol[None, :] + 15, 0, 30)
    for kind in range(3):
        lp = kind
        c0 = max(lp - 2, 0)
        for jj in range(5):
            lc = c0 + jj
            for a in range(2):
                krow = R[2 * lc + a]
                for e in range(2):
                    qrow = R[2 * lp + e]
                    rs = min(max(qrow - 4, 0), 56)
                    if not (rs <= krow <= rs + 7):
                        continue
                    ridx = krow - qrow + 7
                    vals = rpb[:, ridx, :][:, cidx]
                    blk = np.where(colok[None], vals, NEG).astype(np.float32)
                    tab[a * 64:(a + 1) * 64, :, kind, jj * 128 + e * 64:jj * 128 + e * 64 + 64] = blk.transpose(1, 0, 2)
    return np.ascontiguousarray(tab.reshape(128, 8 * 3 * 640))


_DFT_CACHE = {}


def _dft_tables(hf):
    if hf in _DFT_CACHE:
        return _DFT_CACHE[hf]
    R = _rows(hf)
    sl = np.arange(4096)
    pos = 64 * R[sl // 64] + (sl % 64)
    k = np.arange(4096)
    scale = 1.0 / np.sqrt(4096.0 * 128.0)
    cosv = (np.cos(2 * np.pi * k / 4096.0) * scale)
    nsinv = (-np.sin(2 * np.pi * k / 4096.0) * scale)
    prod = (pos[:, None].astype(np.int64) * pos[None, :2048].astype(np.int64)) % 4096
    out = np.empty((32, 128, 2, 4, 512), ml_dtypes.bfloat16)
    for sb in range(4):
        for s8 in range(8):
            blk = sb * 8 + s8
            pr = prod[s8 * 512:(s8 + 1) * 512, sb * 512:(sb + 1) * 512].reshape(4, 128, 512)
            out[blk, :, 0] = cosv[pr].transpose(1, 0, 2).astype(ml_dtypes.bfloat16)
            out[blk, :, 1] = nsinv[pr].transpose(1, 0, 2).astype(ml_dtypes.bfloat16)
    res = np.ascontiguousarray(out.reshape(32, 128, 4096))
    _DFT_CACHE[hf] = res
    return res


def _pp(v, n):
    return np.ascontiguousarray(np.asarray(v, np.float32).reshape(n, 128).T)


_NC_CACHE = {}


def make_in_maps(inputs):
    f = lambda k: np.asarray(inputs[k], np.float32)
    x = f("x")
    c = f("c")
    rpb = f("rpb")[0]
    cc = np.arange(128)
    ang = 2 * np.pi * np.outer(cc, cc) / 128.0
    cst_cs = np.concatenate([np.cos(ang), np.sin(ang)], axis=1).astype(np.float32)
    bones = np.zeros((128, 128), np.float32)
    bones[:64, :64] = 1.0
    bones[64:, 64:] = 1.0
    shared = {
        "w_ada": np.ascontiguousarray(f("w_ada")[0]),
        "b_ada": np.ascontiguousarray(f("b_ada")[0].reshape(1, 6144)),
        "bada_pp": _pp(f("b_ada")[0], 48),
        "n1w_pp": _pp(f("norm1_w")[0], 8),
        "n2w_pp": _pp(f("norm2_w")[0], 8),
        "w_in": np.ascontiguousarray(f("w_in")[0]),
        "bin_pp": _pp(f("b_in")[0], 32),
        "qkw": np.ascontiguousarray(np.stack([np.tile(f("q_norm_w")[0], 2), np.tile(f("k_norm_w")[0], 2)], axis=1)),
        "w_na": np.ascontiguousarray(f("w_na_out")[0]),
        "w_fn": np.ascontiguousarray(f("w_fn_out")[0]),
        "bfn_pp": _pp(f("b_fn_out")[0], 8),
        "w_o": np.ascontiguousarray(f("w_o")[0]),
        "w1": np.ascontiguousarray(f("w_mlp_in")[0]),
        "w2": np.ascontiguousarray(f("w_mlp_out")[0]),
        "cst_ident": np.eye(128, dtype=np.float32),
        "cst_bones": bones,
        "cst_cs": cst_cs,
    }
    btabs = [_bias_tables(rpb, hf) for hf in range(2)]
    in_maps = []
    for core in range(8):
        b, hf = core // 2, core % 2
        R = _rows(hf)
        xl = np.ascontiguousarray(x[b].reshape(64, 64, 1024)[R].reshape(4096, 1024))
        m = dict(shared)
        m["x"] = xl
        m["c_pp"] = _pp(c[b], 8)
        m["btab"] = btabs[hf]
        m["dft"] = _dft_tables(hf)
        in_maps.append(m)
    return in_maps


def assemble(results, dtype=np.float32):
    out = np.empty((4, 4096, 1024), dtype)
    for core in range(8):
        b, hf = core // 2, core % 2
        R = _rows(hf)[:32]
        o = np.asarray(results[core]["out"]).reshape(32, 64, 1024)
        out[b].reshape(64, 64, 1024)[R] = o
    return out


def kernel(**inputs):
    if "nc" not in _NC_CACHE:
        _NC_CACHE["nc"] = build_program()
    nc = _NC_CACHE["nc"]
    in_maps = make_in_maps(inputs)
    res = run_bass_kernel_spmd(nc, in_maps, core_ids=list(range(8)))
    return assemble(res.results)
```

```python
import contextlib
import numpy as np
import ml_dtypes
import concourse.bass as bass
import concourse.mybir as mybir
from concourse.bass_utils import run_bass_kernel_spmd

F32 = mybir.dt.float32
BF16 = mybir.dt.bfloat16
AF = mybir.ActivationFunctionType
ALU = mybir.AluOpType
AX = mybir.AxisListType
EPS = 1e-6
KB = 1024
ARENA_BYTES = 206 * KB
NEG = -30000.0


class Tile:
    __slots__ = ("name", "w", "rs")

    def __init__(self, name=""):
        self.name = name
        self.w = None
        self.rs = []


class Op:
    __slots__ = ("eng", "fn", "deps", "dma", "sig", "sem", "val", "prev_dma")

    def __init__(self, eng, fn, deps, dma):
        self.eng = eng
        self.fn = fn
        self.deps = deps
        self.dma = dma
        self.sig = False
        self.sem = None
        self.val = 0
        self.prev_dma = None


class Sched:
    ENGS = ("pe", "act", "dve", "pool", "sp")

    def __init__(self, n_dma_sems=16, same_engine_sync=True):
        self.q = {e: [] for e in self.ENGS}
        self.n_dma_sems = n_dma_sems
        self.same_engine_sync = same_engine_sync
        self.dma_ops = []
        self.dma_since_barrier = []
        self.dma_carry = []
        self.last_real = {}

    def op(self, eng, fn, reads=(), writes=(), dma=False, extra_deps=(), nobarrier=False):
        deps = set(extra_deps)
        for t in reads:
            if t.w is not None:
                deps.add(t.w)
        for t in writes:
            if t.w is not None:
                deps.add(t.w)
            deps.update(t.rs)
        o = Op(eng, fn, deps, dma)
        for t in reads:
            t.rs.append(o)
        for t in writes:
            t.w = o
            t.rs = []
        self.q[eng].append(o)
        if fn is not None and not dma:
            self.last_real[eng] = o
        if dma:
            self.dma_ops.append(o)
            if not nobarrier:
                self.dma_since_barrier.append(o)
            else:
                self.dma_carry.append(o)
        return o

    def barrier(self):
        deps = set(self.last_real.values()) | set(self.dma_since_barrier)
        self.dma_since_barrier = list(self.dma_carry)
        self.dma_carry = []
        for e in self.ENGS:
            self.op(e, None, extra_deps=list(deps))

    def _skip(self, d, o):
        if d.dma:
            return False
        if d.eng == "pe" and o.eng == "pe":
            return True
        if (not self.same_engine_sync) and d.eng == o.eng:
            return True
        return False

    def finalize(self, sems):
        for e in self.ENGS:
            for o in self.q[e]:
                for d in o.deps:
                    if not self._skip(d, o):
                        d.sig = True
        for e in self.ENGS:
            cnt = 0
            for o in self.q[e]:
                if o.dma or o.fn is None:
                    continue
                if o.sig:
                    cnt += 1
                    o.sem = sems[e]
                    o.val = cnt
        for qn in ("sp", "pool", "act"):
            pool = sems["dma_" + qn]
            dcount = [0] * len(pool)
            dlast = [None] * len(pool)
            i = 0
            for o in self.dma_ops:
                if o.eng != qn:
                    continue
                s = i % len(pool)
                i += 1
                dcount[s] += 16
                o.sem = pool[s]
                o.val = dcount[s]
                o.prev_dma = dlast[s]
                dlast[s] = o
                o.sig = True

    def replay(self, eng_name, eng):
        seen = {}

        def wait(d):
            if d.sem is None:
                return
            k = id(d.sem)
            if seen.get(k, 0) >= d.val:
                return
            eng.wait_ge(d.sem, d.val)
            seen[k] = d.val

        for o in self.q[eng_name]:
            for d in sorted(o.deps, key=lambda d: d.val):
                if self._skip(d, o):
                    continue
                wait(d)
            if o.dma and o.prev_dma is not None:
                wait(o.prev_dma)
            if o.fn is None:
                continue
            ins = o.fn(eng)
            if o.sig:
                ins.then_inc(o.sem, 16 if o.dma else 1)


def build_program(stop_after=None, same_engine_sync=True):
    nc = bass.Bass("TRN2", target_bir_lowering=False)

    def din(name, shape, dt=F32):
        return nc.dram_tensor(name, list(shape), dt, kind="ExternalInput").ap()

    x_d = din("x", [4096, 1024])
    c_d = din("c_pp", [128, 8])
    wada_d = din("w_ada", [1024, 6144])
    bada_d = din("b_ada", [1, 6144])
    badapp_d = din("bada_pp", [128, 48])
    n1w_d = din("n1w_pp", [128, 8])
    n2w_d = din("n2w_pp", [128, 8])
    win_d = din("w_in", [1024, 4096])
    bin_d = din("bin_pp", [128, 32])
    qkw_d = din("qkw", [128, 2])
    btab_d = din("btab", [128, 8 * 3 * 640])
    wna_d = din("w_na", [512, 1024])
    wfn_d = din("w_fn", [512, 1024])
    bfn_d = din("bfn_pp", [128, 8])
    wo_d = din("w_o", [1024, 1024])
    w1_d = din("w1", [1024, 4096])
    w2_d = din("w2", [4096, 1024])
    cid_d = din("cst_ident", [128, 128])
    cbo_d = din("cst_bones", [128, 128])
    ccs_d = din("cst_cs", [128, 256])
    dft_d = din("dft", [32, 128, 4096], BF16)
    out_d = nc.dram_tensor("out", [2048, 1024], F32, kind="ExternalOutput").ap()

    es = contextlib.ExitStack()
    with es:
        arena = es.enter_context(nc.sbuf_tensor("arena", [128, ARENA_BYTES // 2], BF16))
        PS_all = es.enter_context(nc.psum_tensor("ps_all", [128, 4096], F32))
        PS = [PS_all[:, i * 1024:(i + 1) * 1024] for i in range(4)]
        sems = {e: es.enter_context(nc.semaphore(f"s_{e}")) for e in Sched.ENGS}
        sems["dma_sp"] = [es.enter_context(nc.semaphore(f"sdsp{i}")) for i in range(12)]
        sems["dma_pool"] = [es.enter_context(nc.semaphore(f"sdpl{i}")) for i in range(8)]
        sems["dma_act"] = [es.enter_context(nc.semaphore(f"sdac{i}")) for i in range(4)]
        S = Sched(n_dma_sems=16, same_engine_sync=same_engine_sync)

        def V(off, shape, dt):
            n = int(np.prod(shape[1:]))
            esz = 2 if dt == BF16 else 4
            assert off % 32 == 0 or n * esz < 32 or off % 4 == 0
            assert off + n * esz <= ARENA_BYTES, (off, shape)
            v = arena[0:shape[0], off // 2:(off + n * esz) // 2]
            if dt == F32:
                v = v.bitcast(F32)
            if len(shape) == 3:
                v = v.rearrange("p (a b) -> p a b", a=shape[1])
            elif len(shape) == 4:
                v = v.rearrange("p (a b c) -> p a b c", a=shape[1], b=shape[2])
            return v

        def bank(k):
            return PS[k // 2][:, (k % 2) * 512:(k % 2) * 512 + 512]

        def bank_bf(k):
            return bank(k).bitcast(BF16)

        def bc(ap, shape):
            return ap.to_broadcast(list(shape))

        ident = V(0, [128, 128], BF16)
        identf = V(256, [128, 128], F32)
        bones = V(768, [128, 128], BF16)
        cs_t = V(1024, [128, 256], BF16)
        sm = 1536
        c_sb = V(sm + 0, [128, 8], F32)
        sc_sb = V(sm + 32, [128, 8], F32)
        n1w = V(sm + 64, [128, 8], F32)
        n2w = V(sm + 96, [128, 8], F32)
        gs1 = V(sm + 128, [128, 8], F32)
        sh1 = V(sm + 160, [128, 8], F32)
        gs2 = V(sm + 192, [128, 8], F32)
        sh2 = V(sm + 224, [128, 8], F32)
        bin_pp = V(sm + 256, [128, 32], F32)
        bfn_pp = V(sm + 384, [128, 8], F32)
        qkw = V(sm + 416, [128, 2], F32)
        kw8 = V(sm + 424, [128, 1], F32)
        ppx = V(sm + 448, [128, 32], F32)
        bada_pp = V(sm + 576, [128, 48], F32)
        ones_row = V(2432, [1, 128], F32)
        gate1_bc = V(4 * KB, [128, 1024], F32)
        gate2_bc = V(8 * KB, [128, 1024], F32)
        stat = V(12 * KB, [128, 256], F32)
        t_const = Tile("const")
        t_mods = Tile("mods")
        t_gate = Tile("gate")

        def dma(eng, out, in_, reads=(), writes=(), extra_deps=(), nobarrier=False):
            return S.op(eng, lambda e: e.dma_start(out=out, in_=in_), reads=reads, writes=writes, dma=True,
                        extra_deps=extra_deps, nobarrier=nobarrier)

        t_cs = []

        def cdma(eng, out, in_):
            t = Tile()
            t_cs.append(t)
            dma(eng, out, in_, writes=[t])

        cdma("sp", c_sb, c_d)
        cdma("pool", ident, cid_d)
        cdma("sp", identf, cid_d)
        cdma("sp", bada_pp, badapp_d)
        cdma("sp", n1w, n1w_d)
        cdma("sp", n2w, n2w_d)
        cdma("sp", bin_pp, bin_d)
        cdma("sp", bfn_pp, bfn_d)
        cdma("sp", qkw, qkw_d)
        cdma("pool", bones, cbo_d)
        cdma("pool", cs_t, ccs_d)
        S.op("dve", lambda e: e.memset(stat[:, 208:209], 0.0), reads=t_cs, writes=[t_const])
        S.op("dve", lambda e: e.memset(ones_row, 1.0), writes=[t_const])
        S.op("dve", lambda e: e.tensor_scalar(out=kw8, in0=qkw[:, 1:2], scalar1=8.0, scalar2=None, op0=ALU.mult),
             reads=[t_const], writes=[t_const])

        wada_buf = [V(110 * KB + i * 8 * KB, [128, 8, 512], BF16) for i in range(3)]
        t_wada = [Tile(), Tile(), Tile()]
        screp = V(134 * KB, [128, 8, 128], BF16)
        modrows = V(136 * KB, [128, 2048], F32)
        tmpdiag = V(184 * KB, [128, 8, 128], F32)
        t_screp, t_modrows, t_tmpdiag = Tile(), Tile(), Tile()
        t_bank = [Tile(f"bank{k}") for k in range(8)]
        Wqkvu = V(14 * KB, [128, 8, 2048], BF16)
        t_W = [Tile("Wqkv_a"), Tile("Wqkv_d")]
        t_Wu = Tile("Wu")
        win_v = win_d.rearrange("(kc p) n -> p kc n", p=128)
        wada_v = wada_d.rearrange("(kc p) n -> p kc n", p=128)

        S.op("act", lambda e: e.activation(out=sc_sb, in_=c_sb, func=AF.Silu), reads=[t_const], writes=[t_screp])
        S.op("dve", lambda e: e.tensor_copy(out=screp, in_=bc(sc_sb.unsqueeze(2), [128, 8, 128])),
             reads=[t_screp], writes=[t_screp])
        dma("sp", gate1_bc, bada_d[:, 2048:3072].partition_broadcast(128), writes=[t_gate])
        dma("sp", gate2_bc, bada_d[:, 5120:6144].partition_broadcast(128), writes=[t_gate])
        for nb in range(4):
            b = nb % 3
            dma("pool", wada_buf[b], wada_v[:, :, nb * 512:(nb + 1) * 512], writes=[t_wada[b]])

            def mm(e, nb=nb, b=b):
                ps = bank(nb % 2)
                ins = None
                for kc in range(8):
                    ins = e.matmul(ps, lhsT=screp[:, kc, :], rhs=wada_buf[b][:, kc, :], start=(kc == 0),
                                   stop=(kc == 7))
                return ins
            S.op("pe", mm, reads=[t_screp, t_wada[b]], writes=[t_bank[nb % 2]])
            S.op("act", lambda e, nb=nb: e.activation(out=modrows[:, nb * 512:(nb + 1) * 512], in_=bank(nb % 2),
                                                      func=AF.Copy),
                 reads=[t_bank[nb % 2]], writes=[t_modrows])
        for kc in range(8):
            dma("pool", Wqkvu[:, kc, 1536:2048], win_v[:, kc, 1536:2048], writes=[t_Wu], nobarrier=True)
        stg_w = [V(46 * KB + kc * 6 * KB, [128, 1536], F32) for kc in range(8)]
        t_stgw = [Tile() for _ in range(8)]
        for kc in range(8):
            dma("sp", stg_w[kc], win_v[:, kc, 0:1536], writes=[t_stgw[kc]], nobarrier=True)
        for i in range(2):
            src = modrows[:, i * 1024:(i + 1) * 1024].rearrange("p (a b) -> p a b", a=8)
            S.op("dve", lambda e, src=src: e.tensor_tensor(out=tmpdiag, in0=src,
                                                            in1=bc(identf.unsqueeze(1), [128, 8, 128]), op=ALU.mult),
                 reads=[t_modrows, t_const], writes=[t_tmpdiag])
            S.op("dve", lambda e, i=i: e.tensor_reduce(out=ppx[:, i * 8:(i + 1) * 8], in_=tmpdiag, axis=AX.X,
                                                       op=ALU.add),
                 reads=[t_tmpdiag], writes=[t_mods])
            S.op("dve", lambda e, i=i: e.tensor_tensor(out=ppx[:, i * 8:(i + 1) * 8], in0=ppx[:, i * 8:(i + 1) * 8],
                                                       in1=bada_pp[:, i * 8:(i + 1) * 8], op=ALU.add),
                 reads=[t_mods, t_const], writes=[t_mods])

        def finish_mods(gs, sh, nw, i_sh, i_sc):
            S.op("dve", lambda e: e.scalar_tensor_tensor(
                out=gs, in0=ppx[:, i_sc * 8:(i_sc + 1) * 8], scalar=1.0, in1=nw, op0=ALU.add, op1=ALU.mult),
                reads=[t_mods, t_const], writes=[t_mods])
            S.op("dve", lambda e: e.tensor_scalar(out=gs, in0=gs, scalar1=32.0, scalar2=None, op0=ALU.mult),
                 reads=[t_mods], writes=[t_mods])
            S.op("dve", lambda e: e.tensor_copy(out=sh, in_=ppx[:, i_sh * 8:(i_sh + 1) * 8]),
                 reads=[t_mods], writes=[t_mods])

        finish_mods(gs1, sh1, n1w, 0, 1)
        S.barrier()

        def prep_A(src_ap, xin, t_xin, xs, t_xs, junk, t_junk, st, t_st, dma_src=True, src_tiles=(),
                   scale_eng="act"):
            if dma_src:
                dma("sp", xin, src_ap, writes=[t_xin])
                rd = [t_xin]
            else:
                xin = src_ap
                rd = list(src_tiles)
            S.op("act", lambda e: e.activation(out=junk, in_=xin, func=AF.Square, accum_out=st[:, 0:1]),
                 reads=rd, writes=[t_junk, t_st])
            S.op("act", lambda e: e.activation(out=st[:, 1:2], in_=st[:, 0:1], func=AF.Ln, scale=1.0,
                                               bias=st[:, 3:4]),
                 reads=[t_st], writes=[t_st])
            S.op("act", lambda e: e.activation(out=st[:, 2:3], in_=st[:, 1:2], func=AF.Exp, scale=-0.5),
                 reads=[t_st], writes=[t_st])
            if scale_eng == "act":
                S.op("act", lambda e: e.activation(out=xs, in_=xin, func=AF.Copy, scale=st[:, 2:3]),
                     reads=rd + [t_st], writes=[t_xs])
            else:
                S.op("pool", lambda e: e.tensor_scalar(out=xs, in0=xin, scalar1=st[:, 2:3], scalar2=None,
                                                       op0=ALU.mult),
                     reads=rd + [t_st], writes=[t_xs])

        def prep_B(xs, t_xs, tpk, tmpT, t_tmpT, dst, t_dst, gs, sh, extra_deps=()):
            tp = bank_bf(tpk).rearrange("p (a b) -> p a b", a=8)

            def tr(e):
                ins = None
                for kc in range(8):
                    ins = e.transpose(out=tp[:, kc, :], in_=xs[:, kc * 128:(kc + 1) * 128], identity=ident)
                return ins
            S.op("pe", tr, reads=[t_xs, t_const], writes=[t_bank[tpk]])
            S.op("dve", lambda e: e.tensor_tensor(out=tmpT, in0=tp, in1=bc(gs.unsqueeze(2), [128, 8, 128]),
                                                  op=ALU.mult),
                 reads=[t_bank[tpk], t_mods], writes=[t_tmpT])
            S.op("dve", lambda e: e.tensor_tensor(out=dst, in0=tmpT, in1=bc(sh.unsqueeze(2), [128, 8, 128]),
                                                  op=ALU.add),
                 reads=[t_tmpT, t_mods], writes=[t_dst], extra_deps=extra_deps)

        NSTAT = 8
        t_stat = [Tile(f"stat{i}") for i in range(NSTAT)]
        for i in range(NSTAT):
            S.op("dve", lambda e, i=i: e.memset(stat[:, i * 8 + 3:i * 8 + 4], 1024.0 * EPS), writes=[t_stat[i]])
        eps64 = stat[:, 200:201]
        S.op("dve", lambda e: e.memset(eps64, 64.0 * EPS), writes=[t_const])

        qT = V(46 * KB, [128, 4, 2048], BF16)
        kT = V(62 * KB, [128, 4, 2304], BF16)
        Vaug = V(80 * KB, [128, 18, 8, 65], BF16)
        hT_own = V(99 * KB, [128, 8, 2048], BF16)
        uT = V(131 * KB, [128, 4, 4096], BF16)
        xin = [V(163 * KB, [128, 1024], F32), V(167 * KB, [128, 1024], F32)]
        xs = [V(171 * KB + i * 2 * KB, [128, 1024], BF16) for i in range(4)]
        junk = V(179 * KB, [128, 1024], BF16)
        tmpT = V(181 * KB, [128, 8, 128], F32)
        hTs_x = [V(185 * KB, [128, 8, 512], BF16)]
        zq = [V(193 * KB + i * 2 * KB, [128, 512], F32) for i in range(3)]
        sq = [V(199 * KB + i * KB, [128, 512], BF16) for i in range(3)]
        rstd = [V(202 * KB + i * 2 * KB, [128, 512], F32) for i in range(2)]
        t_qT = [Tile() for _ in range(4)]
        t_kT = [Tile() for _ in range(5)]
        t_Vaug = [Tile() for _ in range(18)]
        t_hTo = [Tile() for _ in range(4)]
        t_uT = [Tile() for _ in range(8)]
        t_xin = [Tile(), Tile()]
        t_xs = [Tile() for _ in range(4)]
        t_junk, t_tmpT = Tile(), Tile()
        t_hTsx = [Tile()]
        t_zq, t_sq, t_rstd = [Tile() for _ in range(3)], [Tile() for _ in range(3)], [Tile(), Tile()]


        mm_banks = [2, 3, 4, 5]
        mm_ctr = [0]

        def next_bank():
            k = mm_banks[mm_ctr[0] % len(mm_banks)]
            mm_ctr[0] += 1
            return k

        def hts_of(Sx):
            if Sx < 4:
                return hT_own[:, :, Sx * 512:(Sx + 1) * 512], t_hTo[Sx]
            return hTs_x[0], t_hTsx[0]

        def emit_A_tile(Sx, tl):
            g = 4 * Sx + tl
            prep_A(x_d[g * 128:(g + 1) * 128, :], xin[g % 2], t_xin[g % 2], xs[tl], t_xs[tl], junk, t_junk,
                   stat[:, (g % NSTAT) * 8:(g % NSTAT) * 8 + 8], t_stat[g % NSTAT])

        def emit_B_tile(Sx, tl):
            hts, t_h = hts_of(Sx)
            g = 4 * Sx + tl
            prep_B(xs[tl], t_xs[tl], g % 2, tmpT, t_tmpT, hts[:, :, tl * 128:(tl + 1) * 128], t_h, gs1, sh1)

        def emit_B(Sx):
            for tl in range(4):
                emit_B_tile(Sx, tl)

        def proj_fm(col0, hts, t_h, n=512, t_w=None):
            t_w = [t_w] if t_w is not None else t_W
            k = next_bank()

            def mm(e):
                ins = None
                for kc in range(8):
                    ins = e.matmul(bank(k)[:, 0:n], lhsT=Wqkvu[:, kc, col0:col0 + 128],
                                   rhs=hts[:, kc, 0:n], start=(kc == 0), stop=(kc == 7))
                return ins
            S.op("pe", mm, reads=list(t_w) + [t_h], writes=[t_bank[k]])
            return k

        qk_ctr = [0]

        def qk_stage1(k, n, bias_col):
            zb = qk_ctr[0] % 3
            qk_ctr[0] += 1
            S.op("dve", lambda e: e.tensor_scalar(out=zq[zb][:, 0:n], in0=bank(k)[:, 0:n],
                                                  scalar1=bin_pp[:, bias_col:bias_col + 1], scalar2=None,
                                                  op0=ALU.add),
                 reads=[t_bank[k], t_const], writes=[t_zq[zb]])
            S.op("act", lambda e: e.activation(out=sq[zb][:, 0:n], in_=zq[zb][:, 0:n], func=AF.Square),
                 reads=[t_zq[zb]], writes=[t_sq[zb]])
            return zb

        def qk_stage2(zb, n, wcol, dst, t_dst):
            k2 = 6 + (zb % 2)
            rb = zb % 2
            S.op("pe", lambda e: e.matmul(bank(k2)[:, 0:n], lhsT=bones, rhs=sq[zb][:, 0:n], start=True, stop=True),
                 reads=[t_sq[zb], t_const], writes=[t_bank[k2]])
            S.op("act", lambda e: e.activation(out=rstd[rb][:, 0:n], in_=bank(k2)[:, 0:n], func=AF.Ln, scale=1.0,
                                               bias=eps64),
                 reads=[t_bank[k2], t_const], writes=[t_rstd[rb]])
            S.op("act", lambda e: e.activation(out=rstd[rb][:, 0:n], in_=rstd[rb][:, 0:n], func=AF.Exp, scale=-0.5),
                 reads=[t_rstd[rb]], writes=[t_rstd[rb]])
            S.op("dve", lambda e: e.scalar_tensor_tensor(out=dst, in0=zq[zb][:, 0:n], scalar=wcol,
                                                         in1=rstd[rb][:, 0:n], op0=ALU.mult, op1=ALU.mult),
                 reads=[t_zq[zb], t_rstd[rb], t_const], writes=[t_dst])

        def emit_mm(Sx, nxt):
            prefetch = nxt is not None
            hts, t_h = hts_of(Sx)
            items = []
            for idx in range(4):
                items.append(("u", idx))
                if Sx < 4:
                    items.append(("q", idx))
                if Sx < 5:
                    items.append(("k", idx))
                    if Sx < 4 or idx < 2:
                        items.append(("v", idx))
            nk = 512 if Sx < 4 else 256
            LAG = 2
            pend = {}
            npf = 0
            nbf = 0
            inter_B = prefetch and not (Sx >= 4 and nxt >= 4)
            for i in range(len(items) + LAG):
                if prefetch and npf < 4 and i >= (npf * max(len(items) - 4, 0)) // 4:
                    emit_A_tile(nxt, npf)
                    npf += 1
                if inter_B and nbf < npf and i >= (nbf * max(len(items) - 4, 0)) // 4 + 3:
                    emit_B_tile(nxt, nbf)
                    nbf += 1
                if i < len(items):
                    kind, idx = items[i]
                    if kind == "u":
                        k = proj_fm(1536 + idx * 128, hts, t_h, t_w=t_Wu)
                        S.op("dve", lambda e, k=k, gq=idx: e.tensor_scalar(
                            out=uT[:, gq, Sx * 512:(Sx + 1) * 512], in0=bank(k), scalar1=bin_pp[:, 12 + gq:13 + gq],
                            scalar2=None, op0=ALU.add),
                            reads=[t_bank[k], t_const], writes=[t_uT[Sx]])
                    elif kind == "q":
                        k = proj_fm(idx * 128, hts, t_h)
                        pend[i] = (qk_stage1(k, 512, idx), 512, qkw[:, 0:1],
                                   qT[:, idx, Sx * 512:(Sx + 1) * 512], t_qT[Sx])
                    elif kind == "k":
                        k = proj_fm(512 + idx * 128, hts, t_h, nk)
                        pend[i] = (qk_stage1(k, nk, 4 + idx), nk, kw8[:, 0:1],
                                   kT[:, idx, Sx * 512:Sx * 512 + nk], t_kT[Sx])
                    else:
                        g = 4 * Sx + idx
                        k = next_bank()

                        def mmv(e, k=k, tl=idx):
                            ins = None
                            for kc in range(8):
                                ins = e.matmul(bank(k), lhsT=hts[:, kc, tl * 128:(tl + 1) * 128],
                                               rhs=Wqkvu[:, kc, 1024:1536], start=(kc == 0), stop=(kc == 7))
                            return ins
                        S.op("pe", mmv, reads=t_W + [t_h], writes=[t_bank[k]])
                        S.op("dve", lambda e, k=k, g=g: e.tensor_copy(
                            out=Vaug[:, g, :, 0:64], in_=bank(k).rearrange("p (a b) -> p a b", a=8)),
                            reads=[t_bank[k]], writes=[t_Vaug[g]])
                j = i - LAG
                if j in pend:
                    qk_stage2(*pend.pop(j))
            while prefetch and npf < 4:
                emit_A_tile(nxt, npf)
                npf += 1
            if prefetch:
                while nbf < 4:
                    emit_B_tile(nxt, nbf)
                    nbf += 1

        order = [5, 0, 6, 1, 7, 2, 4, 3]
        for tl in range(4):
            emit_A_tile(order[0], tl)
        emit_B(order[0])
        cast_ops = []
        for kc in range(8):
            if kc % 2 == 0:
                cast_ops.append(S.op("act", lambda e, kc=kc: e.activation(out=Wqkvu[:, kc, 0:1536], in_=stg_w[kc],
                                                                          func=AF.Copy),
                                     reads=[t_stgw[kc]], writes=[t_W[0]]))
            else:
                cast_ops.append(S.op("dve", lambda e, kc=kc: e.tensor_copy(out=Wqkvu[:, kc, 0:1536], in_=stg_w[kc]),
                                     reads=[t_stgw[kc]], writes=[t_W[1]]))
        S.op("dve", lambda e: e.memset(Vaug[:, :, :, 64:65], 1.0), writes=t_Vaug, extra_deps=cast_ops)
        for i, Sx in enumerate(order):
            emit_mm(Sx, order[i + 1] if i + 1 < len(order) else None)
        S.barrier()

        btab = V(14 * KB, [128, 8, 3, 640], BF16)
        attT = V(163 * KB, [128, 4, 2048], BF16)
        NSB = 4
        tmpS = [V(179 * KB + i * 1280, [128, 640], BF16) for i in range(NSB)]
        Pb = [V(184 * KB + i * 1280, [128, 640], BF16) for i in range(NSB)]
        att = [V(191 * KB, [128, 8, 64], BF16), V(192 * KB, [128, 8, 64], BF16)]
        rec = [V(193 * KB, [128, 8], F32), V(193 * KB + 32, [128, 8], F32)]
        t_attT = [Tile() for _ in range(16)]
        t_tmpS = [Tile() for _ in range(NSB)]
        t_Pb = [Tile() for _ in range(NSB)]
        t_att = [Tile(), Tile()]
        t_rec = [Tile(), Tile()]
        t_S = [Tile(), Tile()]
        t_O = [Tile(), Tile()]
        t_tpa = Tile()
        S_ps = [PS[0], PS[1]]
        btab_v = btab.rearrange("p a b c -> p (a b c)")
        t_btab = [[Tile() for _ in range(4)] for _ in range(3)]
        btab_d4 = btab_d.rearrange("p (a b c) -> p a b c", a=8, b=3)
        def btab_load(kd, hq):
            hs = slice(2 * hq, 2 * hq + 2)
            dma("pool", btab[:, hs, kd, :], btab_d4[:, hs, kd, :], writes=[t_btab[kd][hq]])

        def btab_exp(kd, hq):
            hs = slice(2 * hq, 2 * hq + 2)
            S.op("act", lambda e: e.activation(out=btab[:, hs, kd, :], in_=btab[:, hs, kd, :], func=AF.Exp),
                 reads=[t_btab[kd][hq]], writes=[t_btab[kd][hq]])

        for kd in range(3):
            for hq in range(4):
                btab_load(kd, hq)
        for hq in range(4):
            btab_exp(0, hq)
        Ops = [PS[2][:, 0:260].rearrange("p (a b) -> p a b", a=4),
               PS[2][:, 512:772].rearrange("p (a b) -> p a b", a=4)]
        tpa = bank_bf(6)[:, 0:512].rearrange("p (a b) -> p a b", a=4)

        def att_S(lp, h, it):
            c0 = max(lp - 2, 0)
            hp, h2 = h // 2, h % 2
            pr = slice(64 * h2, 64 * h2 + 64)
            sbi = it % NSB
            spi = it % 2
            Sps = S_ps[spi]

            def mm(e):
                ins = None
                for jj in range(5):
                    ins = e.matmul(Sps[:, jj * 128:(jj + 1) * 128],
                                   lhsT=kT[pr, hp, (c0 + jj) * 128:(c0 + jj + 1) * 128],
                                   rhs=qT[pr, hp, lp * 128:(lp + 1) * 128], start=True, stop=True)
                return ins
            S.op("pe", mm, reads=t_kT + t_qT, writes=[t_S[spi]])
            kind = min(lp, 2)
            S.op("act", lambda e: e.activation(out=tmpS[sbi], in_=Sps[:, 0:640], func=AF.Exp),
                 reads=[t_S[spi]], writes=[t_tmpS[sbi]])
            S.op("dve", lambda e: e.tensor_tensor(out=Pb[sbi], in0=tmpS[sbi], in1=btab[:, h, kind, :],
                                                  op=ALU.mult),
                 reads=[t_tmpS[sbi], t_btab[kind][h // 2]], writes=[t_Pb[sbi]])

        def att_PV(lp, h, it):
            c0 = max(lp - 2, 0)
            sbi = it % NSB

            def mm(e):
                ins = None
                for jj in range(5):
                    ins = e.matmul(Ops[h // 4][:, h % 4, :], lhsT=Pb[sbi][:, jj * 128:(jj + 1) * 128],
                                   rhs=Vaug[:, c0 + jj, h, :], start=(jj == 0), stop=(jj == 4))
                return ins
            S.op("pe", mm, reads=[t_Pb[sbi]] + t_Vaug, writes=[t_O[h // 4]])

        def att_fin(lp, halves=(0, 1)):
            b = lp % 2
            for half in halves:
                S.op("dve", lambda e, half=half: e.reciprocal(out=rec[b][:, half * 4:half * 4 + 4],
                                                              in_=Ops[half][:, :, 64]),
                     reads=[t_O[half]], writes=[t_rec[b]])
                S.op("dve", lambda e, half=half: e.tensor_tensor(
                    out=att[b][:, half * 4:half * 4 + 4, :], in0=Ops[half][:, :, 0:64],
                    in1=bc(rec[b][:, half * 4:half * 4 + 4].unsqueeze(2), [128, 4, 64]), op=ALU.mult),
                    reads=[t_O[half], t_rec[b]], writes=[t_att[b]])

        def att_fin_B(lp):
            b = lp % 2
            attf = att[b].rearrange("p a b -> p (a b)")

            def tr(e):
                ins = None
                for hp in range(4):
                    ins = e.transpose(out=tpa[:, hp, :], in_=attf[:, hp * 128:(hp + 1) * 128], identity=ident)
                return ins
            S.op("pe", tr, reads=[t_att[b], t_const], writes=[t_tpa])
            S.op("dve", lambda e: e.tensor_tensor(out=attT[:, :, lp * 128:(lp + 1) * 128], in0=tpa,
                                                  in1=bc(bin_pp[:, 8:12].unsqueeze(2), [128, 4, 128]), op=ALU.add),
                 reads=[t_tpa, t_const], writes=[t_attT[lp]])

        screp2 = V(44 * KB, [128, 8, 128], BF16)
        wada2 = V(194 * KB, [128, 8, 512], BF16)
        modblk = V(202 * KB, [128, 512], F32)
        tmpd2 = V(204 * KB, [128, 4, 128], F32)
        t_screp2, t_wada2, t_modblk, t_tmpd2, t_b7 = Tile(), Tile(), Tile(), Tile(), Tile()
        S.op("dve", lambda e: e.tensor_copy(out=screp2, in_=bc(sc_sb.unsqueeze(2), [128, 8, 128])),
             reads=[t_const], writes=[t_screp2])

        def p0b_load(nb):
            dma("pool", wada2, wada_v[:, :, nb * 512:(nb + 1) * 512], writes=[t_wada2])

        def p0b_block(nb):
            sec, half = nb // 2, nb % 2

            def mm(e):
                ins = None
                for kc in range(8):
                    ins = e.matmul(bank(7), lhsT=screp2[:, kc, :], rhs=wada2[:, kc, :], start=(kc == 0),
                                   stop=(kc == 7))
                return ins
            S.op("pe", mm, reads=[t_screp2, t_wada2], writes=[t_b7])
            if sec in (2, 5):
                gbc = gate1_bc if sec == 2 else gate2_bc
                dst = gbc[:, half * 512:(half + 1) * 512]
                S.op("dve", lambda e: e.tensor_tensor(out=dst, in0=bank(7), in1=dst, op=ALU.add),
                     reads=[t_b7, t_gate], writes=[t_gate])
            else:
                i = sec - 1
                S.op("act", lambda e: e.activation(out=modblk, in_=bank(7), func=AF.Copy),
                     reads=[t_b7], writes=[t_modblk])
                S.op("dve", lambda e: e.tensor_tensor(out=tmpd2, in0=modblk.rearrange("p (a b) -> p a b", a=4),
                                                      in1=bc(identf.unsqueeze(1), [128, 4, 128]), op=ALU.mult),
                     reads=[t_modblk, t_const], writes=[t_tmpd2])
                c0_ = i * 8 + half * 4
                S.op("dve", lambda e: e.tensor_reduce(out=ppx[:, c0_:c0_ + 4], in_=tmpd2, axis=AX.X, op=ALU.add),
                     reads=[t_tmpd2], writes=[t_mods])
                S.op("dve", lambda e: e.tensor_tensor(out=ppx[:, c0_:c0_ + 4], in0=ppx[:, c0_:c0_ + 4],
                                                      in1=bada_pp[:, sec * 8 + half * 4:sec * 8 + half * 4 + 4],
                                                      op=ALU.add),
                     reads=[t_mods, t_const], writes=[t_mods])

        items = [(lp, h) for lp in range(16) for h in range(8)]
        AHEAD = 2
        for j in range(AHEAD):
            att_S(items[j][0], items[j][1], j)
        for it, (lp, h) in enumerate(items):
            if it % 2 == 0:
                for d in (AHEAD, AHEAD + 1):
                    if it + d < len(items):
                        att_S(items[it + d][0], items[it + d][1], it + d)
            att_PV(lp, h, it)
            if 2 <= it < 6:
                btab_exp(1, it - 2)
            if 9 <= it < 13:
                btab_exp(2, it - 9)
            if h == 3:
                att_fin(lp, (0,))
            if h == 7:
                att_fin(lp, (1,))
            if h == 3 and lp > 0:
                att_fin_B(lp - 1)
            if it % 14 == 0 and it // 14 < 8:
                p0b_load(4 + it // 14)
            if it % 14 == 10 and it // 14 < 8:
                p0b_block(4 + it // 14)
        att_fin_B(15)
        finish_mods(gs2, sh2, n2w, 2, 3)
        S.barrier()

        Vt = V(14 * KB, [128, 32, 1024], BF16)
        YT = V(78 * KB, [128, 4, 2048], BF16)
        dtab = [V(179 * KB, [128, 2, 4, 512], BF16), V(187 * KB, [128, 2, 4, 512], BF16)]
        t_Vt = [Tile() for _ in range(32)]
        t_YT = [Tile() for _ in range(4)]
        t_dtab = [Tile(), Tile()]
        t_PS = [Tile() for _ in range(4)]
        for g in range(32):
            vp = PS[g % 2]

            def mm(e, g=g, vp=vp):
                ins = None
                for gq in range(4):
                    ins = e.matmul(vp[:, gq * 256:(gq + 1) * 256], lhsT=uT[:, gq, g * 128:(g + 1) * 128], rhs=cs_t,
                                   start=True, stop=True)
                return ins
            last_v = S.op("pe", mm, reads=t_uT + [t_const], writes=[t_PS[g % 2]])
            if g % 2 == 0:
                S.op("act", lambda e, g=g, vp=vp: e.activation(out=Vt[:, g, :], in_=vp, func=AF.Copy),
                     reads=[t_PS[g % 2]], writes=[t_Vt[g]])
            else:
                S.op("dve", lambda e, g=g, vp=vp: e.tensor_copy(out=Vt[:, g, :], in_=vp),
                     reads=[t_PS[g % 2]], writes=[t_Vt[g]])
        Wg = V(131 * KB, [128, 8, 2048], BF16)
        Wna = V(195 * KB, [128, 4, 1024], BF16)
        t_Wg, t_Wna, t_Wfn = Tile(), Tile(), Tile()
        for kc in range(8):
            dma("pool", Wg[:, kc, :], win_v[:, kc, 2048:4096], writes=[t_Wg], extra_deps=[last_v])
        dma("pool", Wna, wna_d.rearrange("(a p) n -> p a n", p=128), writes=[t_Wna])
        t_acc = [Tile() for _ in range(4)]
        Wfn_pre = V(46 * KB, [128, 4, 1024], BF16)
        for sb in range(4):
            for sl in range(8):
                blk = sb * 8 + sl
                db = blk % 2
                dma("sp", dtab[db].rearrange("p a b c -> p (a b c)"), dft_d[blk], writes=[t_dtab[db]])

                def mm(e, sl=sl, db=db):
                    ins = None
                    for ti in range(4):
                        g = sl * 4 + ti
                        for gq in range(4):
                            e.matmul(bank(4 + gq), lhsT=Vt[:, g, gq * 256:gq * 256 + 128], rhs=dtab[db][:, 0, ti, :],
                                     start=(g == 0), stop=False)
                            ins = e.matmul(bank(4 + gq), lhsT=Vt[:, g, gq * 256 + 128:gq * 256 + 256],
                                           rhs=dtab[db][:, 1, ti, :], start=False, stop=(g == 31))
                    return ins
                pos_op = S.op("pe", mm, reads=[t_dtab[db]] + t_Vt[sl * 4:sl * 4 + 4], writes=t_acc)
                if sb == 3 and sl == 4:
                    dma("pool", Wfn_pre, wfn_d.rearrange("(a p) n -> p a n", p=128), writes=[t_Wfn],
                        extra_deps=[pos_op])
            for gq in range(4):
                if gq % 2 == 0:
                    S.op("act", lambda e, gq=gq, sb=sb: e.activation(out=YT[:, gq, sb * 512:(sb + 1) * 512],
                                                                     in_=bank(4 + gq), func=AF.Copy),
                         reads=t_acc, writes=[t_YT[sb]])
                else:
                    S.op("dve", lambda e, gq=gq, sb=sb: e.tensor_copy(out=YT[:, gq, sb * 512:(sb + 1) * 512],
                                                                      in_=bank(4 + gq)),
                         reads=t_acc, writes=[t_YT[sb]])
        S.barrier()

        mT = V(14 * KB, [128, 8, 2048], BF16)
        Wfn = V(46 * KB, [128, 4, 1024], BF16)
        sa = [V(54 * KB, [128, 512], F32), V(56 * KB, [128, 512], F32)]
        sbg = [V(58 * KB, [128, 512], F32), V(60 * KB, [128, 512], F32)]
        t1 = [V(62 * KB, [128, 512], F32), V(64 * KB, [128, 512], F32)]
        t2 = [V(66 * KB, [128, 512], F32), V(68 * KB, [128, 512], F32)]
        stgo = [V(70 * KB, [128, 1024], F32), V(74 * KB, [128, 1024], F32)]
        Wo = V(179 * KB, [128, 8, 1024], BF16)
        t_sa, t_sbg, t_t1, t_t2 = [Tile(), Tile()], [Tile(), Tile()], [Tile(), Tile()], [Tile(), Tile()]
        t_mT = [[Tile() for _ in range(4)] for _ in range(8)]
        t_Wo, t_stgo = Tile(), [Tile(), Tile()]
        wo_v = wo_d.rearrange("(kc p) n -> p kc n", p=128)
        it = 0
        n_wo = 0
        for oc in range(8):
            for tb in range(4):
                cols = slice(tb * 512, (tb + 1) * 512)
                st = it % 2
                kb = 4 * st
                it += 1

                def mm(e, oc=oc, cols=cols, kb=kb):
                    for kc in range(8):
                        e.matmul(bank(kb), lhsT=Wg[:, kc, oc * 128:(oc + 1) * 128], rhs=hT_own[:, kc, cols],
                                 start=(kc == 0), stop=(kc == 7))
                    for kc in range(8):
                        e.matmul(bank(kb + 1), lhsT=Wg[:, kc, 1024 + oc * 128:1024 + (oc + 1) * 128],
                                 rhs=hT_own[:, kc, cols], start=(kc == 0), stop=(kc == 7))
                    for hp in range(4):
                        e.matmul(bank(kb + 2), lhsT=Wna[:, hp, oc * 128:(oc + 1) * 128], rhs=attT[:, hp, cols],
                                 start=(hp == 0), stop=(hp == 3))
                    ins = None
                    for gq in range(4):
                        ins = e.matmul(bank(kb + 3), lhsT=Wfn[:, gq, oc * 128:(oc + 1) * 128], rhs=YT[:, gq, cols],
                                       start=(gq == 0), stop=(gq == 3))
                    return ins
                S.op("pe", mm, reads=[t_Wg, t_Wna, t_Wfn] + t_hTo + t_attT + t_YT,
                     writes=[t_PS[2 * st], t_PS[2 * st + 1]])
                S.op("act", lambda e, oc=oc, kb=kb, st=st: e.activation(out=sa[st], in_=bank(kb), func=AF.Sigmoid,
                                                                        bias=bin_pp[:, 16 + oc:17 + oc], scale=1.0),
                     reads=[t_PS[2 * st], t_const], writes=[t_sa[st]])
                S.op("act", lambda e, oc=oc, kb=kb, st=st: e.activation(out=sbg[st], in_=bank(kb + 1), func=AF.Sigmoid,
                                                                        bias=bin_pp[:, 24 + oc:25 + oc], scale=1.0),
                     reads=[t_PS[2 * st], t_const], writes=[t_sbg[st]])
                S.op("dve", lambda e, kb=kb, st=st: e.tensor_tensor(out=t1[st], in0=bank(kb + 2), in1=sa[st],
                                                                    op=ALU.mult),
                     reads=[t_PS[2 * st + 1], t_sa[st]], writes=[t_t1[st]])
                S.op("dve", lambda e, oc=oc, kb=kb, st=st: e.scalar_tensor_tensor(
                    out=t2[st], in0=bank(kb + 3), scalar=bfn_pp[:, oc:oc + 1], in1=sbg[st], op0=ALU.add,
                    op1=ALU.mult),
                    reads=[t_PS[2 * st + 1], t_sbg[st], t_const], writes=[t_t2[st]])
                S.op("pool", lambda e, oc=oc, cols=cols, st=st: e.tensor_tensor(out=mT[:, oc, cols], in0=t1[st],
                                                                                in1=t2[st], op=ALU.add),
                     reads=[t_t1[st], t_t2[st]], writes=[t_mT[oc][tb]])
                if it >= 3 and it % 3 == 0 and n_wo < 8:
                    i = n_wo
                    n_wo += 1
                    dma("sp", stgo[i % 2], wo_v[:, i, :], writes=[t_stgo[i % 2]])
                    S.op("pool", lambda e, i=i: e.tensor_tensor(out=Wo[:, i, :], in0=stgo[i % 2], in1=gate1_bc,
                                                                op=ALU.mult),
                         reads=[t_stgo[i % 2], t_gate], writes=[t_Wo])
        assert n_wo == 8
        S.barrier()
        xr = [V(163 * KB, [128, 1024], F32), V(167 * KB, [128, 1024], F32)]
        acc = V(99 * KB, [128, 16, 1024], F32)
        t_xr = [Tile(), Tile()]
        t_accg = [Tile() for _ in range(16)]
        W1q = [V(46 * KB, [128, 8, 1024], BF16), V(62 * KB, [128, 8, 1024], BF16)]
        W2q = [V(78 * KB, [128, 8, 1024], BF16), V(163 * KB, [128, 8, 1024], BF16)]
        stg2 = [V(195 * KB, [128, 1024], F32), V(199 * KB, [128, 1024], F32)]
        t_W1q, t_W2q = [Tile(), Tile()], [Tile(), Tile()]
        t_stg2 = [Tile(), Tile()]
        w1_v = w1_d.rearrange("(kc p) n -> p kc n", p=128)
        w2_v = w2_d.rearrange("(fc p) n -> p fc n", p=128)

        def load_w1(qt):
            b = qt % 2
            for kc in range(8):
                dma("pool", W1q[b][:, kc, :], w1_v[:, kc, qt * 1024:(qt + 1) * 1024], writes=[t_W1q[b]])

        def load_w2_piece(qt, fc):
            b = qt % 2
            i = qt * 8 + fc
            dma("sp", stg2[i % 2], w2_v[:, i, :], writes=[t_stg2[i % 2]])
            S.op("pool", lambda e: e.tensor_tensor(out=W2q[b][:, fc, :], in0=stg2[i % 2], in1=gate2_bc, op=ALU.mult),
                 reads=[t_stg2[i % 2], t_gate], writes=[t_W2q[b]])

        def load_quarter(qt):
            load_w1(qt)
            for fc in range(8):
                load_w2_piece(qt, fc)

        h2T = V(14 * KB, [128, 8, 2048], BF16)
        xs2 = [V(62 * KB + i * 2 * KB, [128, 1024], BF16) for i in range(3)]
        junk2 = V(68 * KB, [128, 1024], BF16)
        tmpT2 = V(70 * KB, [128, 8, 128], F32)
        t_h2T = [Tile() for _ in range(16)]
        t_xs2 = [Tile(), Tile(), Tile()]
        t_junk2, t_tmpT2 = Tile(), Tile()
        mm_ops = {}

        def emit_h2T(g):
            prep_B(xs2[g % 3], t_xs2[g % 3], 4 + g % 2, tmpT2, t_tmpT2, h2T[:, :, g * 128:(g + 1) * 128], t_h2T[g],
                   gs2, sh2, extra_deps=[mm_ops[g]])

        load_w1(0)
        for g in range(16):
            dma("sp", xr[g % 2], x_d[g * 128:(g + 1) * 128, :], writes=[t_xr[g % 2]])
            Cp = PS[g % 2]

            def mm(e, g=g, Cp=Cp):
                ins = None
                for half in range(2):
                    for oc in range(8):
                        ins = e.matmul(Cp[:, half * 512:(half + 1) * 512], lhsT=mT[:, oc, g * 128:(g + 1) * 128],
                                       rhs=Wo[:, oc, half * 512:(half + 1) * 512], start=(oc == 0), stop=(oc == 7))
                return ins
            mm_op = S.op("pe", mm, reads=[t_Wo] + [t_mT[oc][g // 4] for oc in range(8)], writes=[t_PS[g % 2]])
            S.op("dve", lambda e, g=g, Cp=Cp: e.tensor_tensor(out=acc[:, g, :], in0=Cp, in1=xr[g % 2], op=ALU.add),
                 reads=[t_PS[g % 2], t_xr[g % 2]], writes=[t_accg[g]])
            mm_ops[g] = mm_op
            prep_A(acc[:, g, :], None, None, xs2[g % 3], t_xs2[g % 3], junk2, t_junk2,
                   stat[:, (g % NSTAT) * 8:(g % NSTAT) * 8 + 8], t_stat[g % NSTAT], dma_src=False,
                   src_tiles=[t_accg[g]])
            if g >= 2:
                emit_h2T(g - 2)
            if g >= 8:
                load_w2_piece(0, g - 8)
        emit_h2T(14)
        emit_h2T(15)
        S.barrier()

        aTq = [V(179 * KB, [128, 8, 512], BF16), V(187 * KB, [128, 8, 512], BF16)]
        sqf = [V(94 * KB, [128, 512], F32), V(96 * KB, [128, 512], F32)]
        t_aTq = [Tile(), Tile()]
        t_sqf = [Tile(), Tile()]
        out_ops = []
        fit = [0]

        def mlp_up(qt, tb):
            b = qt % 2
            ab = (qt * 4 + tb) % 2
            for fc in range(8):
                kf = fit[0] % 2
                fit[0] += 1

                def mm(e, fc=fc, kf=kf):
                    ins = None
                    for kc in range(8):
                        ins = e.matmul(bank(kf), lhsT=W1q[b][:, kc, fc * 128:(fc + 1) * 128],
                                       rhs=h2T[:, kc, tb * 512:(tb + 1) * 512], start=(kc == 0), stop=(kc == 7))
                    return ins
                S.op("pe", mm, reads=[t_W1q[b]] + t_h2T[tb * 4:tb * 4 + 4], writes=[t_bank[kf]])
                S.op("act", lambda e, kf=kf: e.activation(out=sqf[kf], in_=bank(kf), func=AF.Square),
                     reads=[t_bank[kf]], writes=[t_sqf[kf]])
                S.op("dve", lambda e, kf=kf, fc=fc: e.scalar_tensor_tensor(
                    out=aTq[ab][:, fc, :], in0=bank(kf), scalar=0.0, in1=sqf[kf], op0=ALU.is_gt, op1=ALU.mult),
                    reads=[t_bank[kf], t_sqf[kf]], writes=[t_aTq[ab]])

        def mlp_down(qt, tb):
            b = qt % 2
            ab = (qt * 4 + tb) % 2
            for tl in range(4):
                g = tb * 4 + tl
                Dp = PS[2 + g % 2]

                def mm2(e, tl=tl, Dp=Dp):
                    ins = None
                    for half in range(2):
                        for fc in range(8):
                            ins = e.matmul(Dp[:, half * 512:(half + 1) * 512],
                                           lhsT=aTq[ab][:, fc, tl * 128:(tl + 1) * 128],
                                           rhs=W2q[b][:, fc, half * 512:(half + 1) * 512], start=(fc == 0),
                                           stop=(fc == 7))
                    return ins
                S.op("pe", mm2, reads=[t_aTq[ab], t_W2q[b]], writes=[t_PS[2 + g % 2]])
                S.op("dve", lambda e, g=g, Dp=Dp: e.tensor_tensor(out=acc[:, g, :], in0=Dp, in1=acc[:, g, :],
                                                                  op=ALU.add),
                     reads=[t_PS[2 + g % 2], t_accg[g]], writes=[t_accg[g]])
                if qt == 3:
                    out_ops.append(dma("sp", out_d[g * 128:(g + 1) * 128, :], acc[:, g, :], reads=[t_accg[g]]))

        steps = [(qt, tb) for qt in range(4) for tb in range(4)]
        load_quarter(1)
        mlp_up(*steps[0])
        for i, (qt, tb) in enumerate(steps):
            if i + 1 < len(steps):
                nq, ntb = steps[i + 1]
                if ntb == 0 and False:
                    pass
                mlp_up(nq, ntb)
            mlp_down(qt, tb)
            if tb == 3 and qt + 2 < 4:
                load_quarter(qt + 2)
        S.barrier()

        S.finalize(sems)
        with nc.Block() as block:
            @block.tensor
            def _(e):
                S.replay("pe", e)

            @block.scalar
            def _(e):
                S.replay("act", e)

            @block.vector
            def _(e):
                S.replay("dve", e)

            @block.gpsimd
            def _(e):
                S.replay("pool", e)

            @block.sync
            def _(e):
                S.replay("sp", e)
    return nc


def _rows(hf):
    return np.arange(64) if hf == 0 else np.arange(63, -1, -1)


def _bias_tables(rpb, hf):
    R = _rows(hf)
    tab = np.full((128, 8, 3, 640), NEG, np.float32)
    qcol = np.arange(64)
    kcol = np.arange(64)
    cs = np.clip(qcol - 8, 0, 48)
    colok = (kcol[:, None] >= cs[None, :]) & (kcol[:, None] <= cs[None, :] + 15)
    cidx = np.clip(kcol[:, None] - qcol[None, :] + 15, 0, 30)
    for kind in range(3):
        lp = kind
        c0 = max(lp - 2, 0)
        for jj in range(5):
            lc = c0 + jj
            for a in range(2):
                krow = R[2 * lc + a]
                for e in range(2):
                    qrow = R[2 * lp + e]
                    rs = min(max(qrow - 4, 0), 56)
                    if not (rs <= krow <= rs + 7):
                        continue
                    ridx = krow - qrow + 7
                    vals = rpb[:, ridx, :][:, cidx]
                    blk = np.where(colok[None], vals, NEG).astype(np.float32)
                    tab[a * 64:(a + 1) * 64, :, kind, jj * 128 + e * 64:jj * 128 + e * 64 + 64] = blk.transpose(1, 0, 2)
    return np.ascontiguousarray(tab.reshape(128, 8 * 3 * 640))


_DFT_CACHE = {}


def _dft_tables(hf):
    if hf in _DFT_CACHE:
        return _DFT_CACHE[hf]
    R = _rows(hf)
    sl = np.arange(4096)
    pos = 64 * R[sl // 64] + (sl % 64)
    k = np.arange(4096)
    scale = 1.0 / np.sqrt(4096.0 * 128.0)
    cosv = (np.cos(2 * np.pi * k / 4096.0) * scale)
    nsinv = (-np.sin(2 * np.pi * k / 4096.0) * scale)
    prod = (pos[:, None].astype(np.int64) * pos[None, :2048].astype(np.int64)) % 4096
    out = np.empty((32, 128, 2, 4, 512), ml_dtypes.bfloat16)
    for sb in range(4):
        for s8 in range(8):
            blk = sb * 8 + s8
            pr = prod[s8 * 512:(s8 + 1) * 512, sb * 512:(sb + 1) * 512].reshape(4, 128, 512)
            out[blk, :, 0] = cosv[pr].transpose(1, 0, 2).astype(ml_dtypes.bfloat16)
            out[blk, :, 1] = nsinv[pr].transpose(1, 0, 2).astype(ml_dtypes.bfloat16)
    res = np.ascontiguousarray(out.reshape(32, 128, 4096))
    _DFT_CACHE[hf] = res
    return res


def _pp(v, n):
    return np.ascontiguousarray(np.asarray(v, np.float32).reshape(n, 128).T)


_NC_CACHE = {}


def make_in_maps(inputs):
    f = lambda k: np.asarray(inputs[k], np.float32)
    x = f("x")
    c = f("c")
    rpb = f("rpb")[0]
    cc = np.arange(128)
    ang = 2 * np.pi * np.outer(cc, cc) / 128.0
    cst_cs = np.concatenate([np.cos(ang), np.sin(ang)], axis=1).astype(np.float32)
    bones = np.zeros((128, 128), np.float32)
    bones[:64, :64] = 1.0
    bones[64:, 64:] = 1.0
    shared = {
        "w_ada": np.ascontiguousarray(f("w_ada")[0]),
        "b_ada": np.ascontiguousarray(f("b_ada")[0].reshape(1, 6144)),
        "bada_pp": _pp(f("b_ada")[0], 48),
        "n1w_pp": _pp(f("norm1_w")[0], 8),
        "n2w_pp": _pp(f("norm2_w")[0], 8),
        "w_in": np.ascontiguousarray(f("w_in")[0]),
        "bin_pp": _pp(f("b_in")[0], 32),
        "qkw": np.ascontiguousarray(np.stack([np.tile(f("q_norm_w")[0], 2), np.tile(f("k_norm_w")[0], 2)], axis=1)),
        "w_na": np.ascontiguousarray(f("w_na_out")[0]),
        "w_fn": np.ascontiguousarray(f("w_fn_out")[0]),
        "bfn_pp": _pp(f("b_fn_out")[0], 8),
        "w_o": np.ascontiguousarray(f("w_o")[0]),
        "w1": np.ascontiguousarray(f("w_mlp_in")[0]),
        "w2": np.ascontiguousarray(f("w_mlp_out")[0]),
        "cst_ident": np.eye(128, dtype=np.float32),
        "cst_bones": bones,
        "cst_cs": cst_cs,
    }
    btabs = [_bias_tables(rpb, hf) for hf in range(2)]
    in_maps = []
    for core in range(8):
        b, hf = core // 2, core % 2
        R = _rows(hf)
        xl = np.ascontiguousarray(x[b].reshape(64, 64, 1024)[R].reshape(4096, 1024))
        m = dict(shared)
        m["x"] = xl
        m["c_pp"] = _pp(c[b], 8)
        m["btab"] = btabs[hf]
        m["dft"] = _dft_tables(hf)
        in_maps.append(m)
    return in_maps


def assemble(results, dtype=np.float32):
    out = np.empty((4, 4096, 1024), dtype)
    for core in range(8):
        b, hf = core // 2, core % 2
        R = _rows(hf)[:32]
        o = np.asarray(results[core]["out"]).reshape(32, 64, 1024)
        out[b].reshape(64, 64, 1024)[R] = o
    return out


def kernel(**inputs):
    if "nc" not in _NC_CACHE:
        _NC_CACHE["nc"] = build_program()
    nc = _NC_CACHE["nc"]
    in_maps = make_in_maps(inputs)
    res = run_bass_kernel_spmd(nc, in_maps, core_ids=list(range(8)))
    return assemble(res.results)
```

```python
import contextlib
import numpy as np
import ml_dtypes
import concourse.bass as bass
import concourse.mybir as mybir
from concourse.bass_utils import run_bass_kernel_spmd

F32 = mybir.dt.float32
BF16 = mybir.dt.bfloat16
AF = mybir.ActivationFunctionType
ALU = mybir.AluOpType
AX = mybir.AxisListType
EPS = 1e-6
KB = 1024
ARENA_BYTES = 206 * KB
NEG = -30000.0


class Tile:
    __slots__ = ("name", "w", "rs")

    def __init__(self, name=""):
        self.name = name
        self.w = None
        self.rs = []


class Op:
    __slots__ = ("eng", "fn", "deps", "dma", "sig", "sem", "val", "prev_dma")

    def __init__(self, eng, fn, deps, dma):
        self.eng = eng
        self.fn = fn
        self.deps = deps
        self.dma = dma
        self.sig = False
        self.sem = None
        self.val = 0
        self.prev_dma = None


class Sched:
    ENGS = ("pe", "act", "dve", "pool", "sp")

    def __init__(self, n_dma_sems=16, same_engine_sync=True):
        self.q = {e: [] for e in self.ENGS}
        self.n_dma_sems = n_dma_sems
        self.same_engine_sync = same_engine_sync
        self.dma_ops = []
        self.dma_since_barrier = []
        self.dma_carry = []
        self.last_real = {}

    def op(self, eng, fn, reads=(), writes=(), dma=False, extra_deps=(), nobarrier=False):
        deps = set(extra_deps)
        for t in reads:
            if t.w is not None:
                deps.add(t.w)
        for t in writes:
            if t.w is not None:
                deps.add(t.w)
            deps.update(t.rs)
        o = Op(eng, fn, deps, dma)
        for t in reads:
            t.rs.append(o)
        for t in writes:
            t.w = o
            t.rs = []
        self.q[eng].append(o)
        if fn is not None and not dma:
            self.last_real[eng] = o
        if dma:
            self.dma_ops.append(o)
            if not nobarrier:
                self.dma_since_barrier.append(o)
            else:
                self.dma_carry.append(o)
        return o

    def barrier(self):
        deps = set(self.last_real.values()) | set(self.dma_since_barrier)
        self.dma_since_barrier = list(self.dma_carry)
        self.dma_carry = []
        for e in self.ENGS:
            self.op(e, None, extra_deps=list(deps))

    def _skip(self, d, o):
        if d.dma:
            return False
        if d.eng == "pe" and o.eng == "pe":
            return True
        if (not self.same_engine_sync) and d.eng == o.eng:
            return True
        return False

    def finalize(self, sems):
        for e in self.ENGS:
            for o in self.q[e]:
                for d in o.deps:
                    if not self._skip(d, o):
                        d.sig = True
        for e in self.ENGS:
            cnt = 0
            for o in self.q[e]:
                if o.dma or o.fn is None:
                    continue
                if o.sig:
                    cnt += 1
                    o.sem = sems[e]
                    o.val = cnt
        for qn in ("sp", "pool", "act"):
            pool = sems["dma_" + qn]
            dcount = [0] * len(pool)
            dlast = [None] * len(pool)
            i = 0
            for o in self.dma_ops:
                if o.eng != qn:
                    continue
                s = i % len(pool)
                i += 1
                dcount[s] += 16
                o.sem = pool[s]
                o.val = dcount[s]
                o.prev_dma = dlast[s]
                dlast[s] = o
                o.sig = True

    def replay(self, eng_name, eng):
        seen = {}

        def wait(d):
            if d.sem is None:
                return
            k = id(d.sem)
            if seen.get(k, 0) >= d.val:
                return
            eng.wait_ge(d.sem, d.val)
            seen[k] = d.val

        for o in self.q[eng_name]:
            for d in sorted(o.deps, key=lambda d: d.val):
                if self._skip(d, o):
                    continue
                wait(d)
            if o.dma and o.prev_dma is not None:
                wait(o.prev_dma)
            if o.fn is None:
                continue
            ins = o.fn(eng)
            if o.sig:
                ins.then_inc(o.sem, 16 if o.dma else 1)


def build_program(stop_after=None, same_engine_sync=True):
    nc = bass.Bass("TRN2", target_bir_lowering=False)

    def din(name, shape, dt=F32):
        return nc.dram_tensor(name, list(shape), dt, kind="ExternalInput").ap()

    x_d = din("x", [4096, 1024])
    c_d = din("c_pp", [128, 8])
    wada_d = din("w_ada", [1024, 6144])
    bada_d = din("b_ada", [1, 6144])
    badapp_d = din("bada_pp", [128, 48])
    n1w_d = din("n1w_pp", [128, 8])
    n2w_d = din("n2w_pp", [128, 8])
    win_d = din("w_in", [1024, 4096])
    bin_d = din("bin_pp", [128, 32])
    qkw_d = din("qkw", [128, 2])
    btab_d = din("btab", [128, 8 * 3 * 640])
    wna_d = din("w_na", [512, 1024])
    wfn_d = din("w_fn", [512, 1024])
    bfn_d = din("bfn_pp", [128, 8])
    wo_d = din("w_o", [1024, 1024])
    w1_d = din("w1", [1024, 4096])
    w2_d = din("w2", [4096, 1024])
    cid_d = din("cst_ident", [128, 128])
    cbo_d = din("cst_bones", [128, 128])
    ccs_d = din("cst_cs", [128, 256])
    dft_d = din("dft", [32, 128, 4096], BF16)
    out_d = nc.dram_tensor("out", [2048, 1024], F32, kind="ExternalOutput").ap()

    es = contextlib.ExitStack()
    with es:
        arena = es.enter_context(nc.sbuf_tensor("arena", [128, ARENA_BYTES // 2], BF16))
        PS_all = es.enter_context(nc.psum_tensor("ps_all", [128, 4096], F32))
        PS = [PS_all[:, i * 1024:(i + 1) * 1024] for i in range(4)]
        sems = {e: es.enter_context(nc.semaphore(f"s_{e}")) for e in Sched.ENGS}
        sems["dma_sp"] = [es.enter_context(nc.semaphore(f"sdsp{i}")) for i in range(12)]
        sems["dma_pool"] = [es.enter_context(nc.semaphore(f"sdpl{i}")) for i in range(8)]
        sems["dma_act"] = [es.enter_context(nc.semaphore(f"sdac{i}")) for i in range(4)]
        S = Sched(n_dma_sems=16, same_engine_sync=same_engine_sync)

        def V(off, shape, dt):
            n = int(np.prod(shape[1:]))
            esz = 2 if dt == BF16 else 4
            assert off % 32 == 0 or n * esz < 32 or off % 4 == 0
            assert off + n * esz <= ARENA_BYTES, (off, shape)
            v = arena[0:shape[0], off // 2:(off + n * esz) // 2]
            if dt == F32:
                v = v.bitcast(F32)
            if len(shape) == 3:
                v = v.rearrange("p (a b) -> p a b", a=shape[1])
            elif len(shape) == 4:
                v = v.rearrange("p (a b c) -> p a b c", a=shape[1], b=shape[2])
            return v

        def bank(k):
            return PS[k // 2][:, (k % 2) * 512:(k % 2) * 512 + 512]

        def bank_bf(k):
            return bank(k).bitcast(BF16)

        def bc(ap, shape):
            return ap.to_broadcast(list(shape))

        ident = V(0, [128, 128], BF16)
        identf = V(256, [128, 128], F32)
        bones = V(768, [128, 128], BF16)
        cs_t = V(1024, [128, 256], BF16)
        sm = 1536
        c_sb = V(sm + 0, [128, 8], F32)
        sc_sb = V(sm + 32, [128, 8], F32)
        n1w = V(sm + 64, [128, 8], F32)
        n2w = V(sm + 96, [128, 8], F32)
        gs1 = V(sm + 128, [128, 8], F32)
        sh1 = V(sm + 160, [128, 8], F32)
        gs2 = V(sm + 192, [128, 8], F32)
        sh2 = V(sm + 224, [128, 8], F32)
        bin_pp = V(sm + 256, [128, 32], F32)
        bfn_pp = V(sm + 384, [128, 8], F32)
        qkw = V(sm + 416, [128, 2], F32)
        kw8 = V(sm + 424, [128, 1], F32)
        ppx = V(sm + 448, [128, 32], F32)
        bada_pp = V(sm + 576, [128, 48], F32)
        ones_row = V(2432, [1, 128], F32)
        gate1_bc = V(4 * KB, [128, 1024], F32)
        gate2_bc = V(8 * KB, [128, 1024], F32)
        stat = V(12 * KB, [128, 256], F32)
        t_const = Tile("const")
        t_mods = Tile("mods")
        t_gate = Tile("gate")

        def dma(eng, out, in_, reads=(), writes=(), extra_deps=(), nobarrier=False):
            return S.op(eng, lambda e: e.dma_start(out=out, in_=in_), reads=reads, writes=writes, dma=True,
                        extra_deps=extra_deps, nobarrier=nobarrier)

        t_cs = []

        def cdma(eng, out, in_):
            t = Tile()
            t_cs.append(t)
            dma(eng, out, in_, writes=[t])

        cdma("sp", c_sb, c_d)
        cdma("pool", ident, cid_d)
        cdma("sp", identf, cid_d)
        cdma("sp", bada_pp, badapp_d)
        cdma("sp", n1w, n1w_d)
        cdma("sp", n2w, n2w_d)
        cdma("sp", bin_pp, bin_d)
        cdma("sp", bfn_pp, bfn_d)
        cdma("sp", qkw, qkw_d)
        cdma("pool", bones, cbo_d)
        cdma("pool", cs_t, ccs_d)
        S.op("dve", lambda e: e.memset(stat[:, 208:209], 0.0), reads=t_cs, writes=[t_const])
        S.op("dve", lambda e: e.memset(ones_row, 1.0), writes=[t_const])
        S.op("dve", lambda e: e.tensor_scalar(out=kw8, in0=qkw[:, 1:2], scalar1=8.0, scalar2=None, op0=ALU.mult),
             reads=[t_const], writes=[t_const])

        wada_buf = [V(110 * KB + i * 8 * KB, [128, 8, 512], BF16) for i in range(3)]
        t_wada = [Tile(), Tile(), Tile()]
        screp = V(134 * KB, [128, 8, 128], BF16)
        modrows = V(136 * KB, [128, 2048], F32)
        tmpdiag = V(184 * KB, [128, 8, 128], F32)
        t_screp, t_modrows, t_tmpdiag = Tile(), Tile(), Tile()
        t_bank = [Tile(f"bank{k}") for k in range(8)]
        Wqkvu = V(14 * KB, [128, 8, 2048], BF16)
        t_W = [Tile("Wqkv_a"), Tile("Wqkv_d")]
        t_Wu = Tile("Wu")
        win_v = win_d.rearrange("(kc p) n -> p kc n", p=128)
        wada_v = wada_d.rearrange("(kc p) n -> p kc n", p=128)

        S.op("act", lambda e: e.activation(out=sc_sb, in_=c_sb, func=AF.Silu), reads=[t_const], writes=[t_screp])
        S.op("dve", lambda e: e.tensor_copy(out=screp, in_=bc(sc_sb.unsqueeze(2), [128, 8, 128])),
             reads=[t_screp], writes=[t_screp])
        dma("sp", gate1_bc, bada_d[:, 2048:3072].partition_broadcast(128), writes=[t_gate])
        dma("sp", gate2_bc, bada_d[:, 5120:6144].partition_broadcast(128), writes=[t_gate])
        for nb in range(4):
            b = nb % 3
            dma("pool", wada_buf[b], wada_v[:, :, nb * 512:(nb + 1) * 512], writes=[t_wada[b]])

            def mm(e, nb=nb, b=b):
                ps = bank(nb % 2)
                ins = None
                for kc in range(8):
                    ins = e.matmul(ps, lhsT=screp[:, kc, :], rhs=wada_buf[b][:, kc, :], start=(kc == 0),
                                   stop=(kc == 7))
                return ins
            S.op("pe", mm, reads=[t_screp, t_wada[b]], writes=[t_bank[nb % 2]])
            S.op("act", lambda e, nb=nb: e.activation(out=modrows[:, nb * 512:(nb + 1) * 512], in_=bank(nb % 2),
                                                      func=AF.Copy),
                 reads=[t_bank[nb % 2]], writes=[t_modrows])
        for kc in range(8):
            dma("pool", Wqkvu[:, kc, 1536:2048], win_v[:, kc, 1536:2048], writes=[t_Wu], nobarrier=True)
        stg_w = [V(46 * KB + kc * 6 * KB, [128, 1536], F32) for kc in range(8)]
        t_stgw = [Tile() for _ in range(8)]
        for kc in range(8):
            dma("sp", stg_w[kc], win_v[:, kc, 0:1536], writes=[t_stgw[kc]], nobarrier=True)
        for i in range(2):
            src = modrows[:, i * 1024:(i + 1) * 1024].rearrange("p (a b) -> p a b", a=8)
            S.op("dve", lambda e, src=src: e.tensor_tensor(out=tmpdiag, in0=src,
                                                            in1=bc(identf.unsqueeze(1), [128, 8, 128]), op=ALU.mult),
                 reads=[t_modrows, t_const], writes=[t_tmpdiag])
            S.op("dve", lambda e, i=i: e.tensor_reduce(out=ppx[:, i * 8:(i + 1) * 8], in_=tmpdiag, axis=AX.X,
                                                       op=ALU.add),
                 reads=[t_tmpdiag], writes=[t_mods])
            S.op("dve", lambda e, i=i: e.tensor_tensor(out=ppx[:, i * 8:(i + 1) * 8], in0=ppx[:, i * 8:(i + 1) * 8],
                                                       in1=bada_pp[:, i * 8:(i + 1) * 8], op=ALU.add),
                 reads=[t_mods, t_const], writes=[t_mods])

        def finish_mods(gs, sh, nw, i_sh, i_sc):
            S.op("dve", lambda e: e.scalar_tensor_tensor(
                out=gs, in0=ppx[:, i_sc * 8:(i_sc + 1) * 8], scalar=1.0, in1=nw, op0=ALU.add, op1=ALU.mult),
                reads=[t_mods, t_const], writes=[t_mods])
            S.op("dve", lambda e: e.tensor_scalar(out=gs, in0=gs, scalar1=32.0, scalar2=None, op0=ALU.mult),
                 reads=[t_mods], writes=[t_mods])
            S.op("dve", lambda e: e.tensor_copy(out=sh, in_=ppx[:, i_sh * 8:(i_sh + 1) * 8]),
                 reads=[t_mods], writes=[t_mods])

        finish_mods(gs1, sh1, n1w, 0, 1)
        S.barrier()

        def prep_A(src_ap, xin, t_xin, xs, t_xs, junk, t_junk, st, t_st, dma_src=True, src_tiles=(),
                   scale_eng="act"):
            if dma_src:
                dma("sp", xin, src_ap, writes=[t_xin])
                rd = [t_xin]
            else:
                xin = src_ap
                rd = list(src_tiles)
            S.op("act", lambda e: e.activation(out=junk, in_=xin, func=AF.Square, accum_out=st[:, 0:1]),
                 reads=rd, writes=[t_junk, t_st])
            S.op("act", lambda e: e.activation(out=st[:, 1:2], in_=st[:, 0:1], func=AF.Ln, scale=1.0,
                                               bias=st[:, 3:4]),
                 reads=[t_st], writes=[t_st])
            S.op("act", lambda e: e.activation(out=st[:, 2:3], in_=st[:, 1:2], func=AF.Exp, scale=-0.5),
                 reads=[t_st], writes=[t_st])
            if scale_eng == "act":
                S.op("act", lambda e: e.activation(out=xs, in_=xin, func=AF.Copy, scale=st[:, 2:3]),
                     reads=rd + [t_st], writes=[t_xs])
            else:
                S.op("pool", lambda e: e.tensor_scalar(out=xs, in0=xin, scalar1=st[:, 2:3], scalar2=None,
                                                       op0=ALU.mult),
                     reads=rd + [t_st], writes=[t_xs])

        def prep_B(xs, t_xs, tpk, tmpT, t_tmpT, dst, t_dst, gs, sh, extra_deps=()):
            tp = bank_bf(tpk).rearrange("p (a b) -> p a b", a=8)

            def tr(e):
                ins = None
                for kc in range(8):
                    ins = e.transpose(out=tp[:, kc, :], in_=xs[:, kc * 128:(kc + 1) * 128], identity=ident)
                return ins
            S.op("pe", tr, reads=[t_xs, t_const], writes=[t_bank[tpk]])
            S.op("dve", lambda e: e.tensor_tensor(out=tmpT, in0=tp, in1=bc(gs.unsqueeze(2), [128, 8, 128]),
                                                  op=ALU.mult),
                 reads=[t_bank[tpk], t_mods], writes=[t_tmpT])
            S.op("dve", lambda e: e.tensor_tensor(out=dst, in0=tmpT, in1=bc(sh.unsqueeze(2), [128, 8, 128]),
                                                  op=ALU.add),
                 reads=[t_tmpT, t_mods], writes=[t_dst], extra_deps=extra_deps)

        NSTAT = 8
        t_stat = [Tile(f"stat{i}") for i in range(NSTAT)]
        for i in range(NSTAT):
            S.op("dve", lambda e, i=i: e.memset(stat[:, i * 8 + 3:i * 8 + 4], 1024.0 * EPS), writes=[t_stat[i]])
        eps64 = stat[:, 200:201]
        S.op("dve", lambda e: e.memset(eps64, 64.0 * EPS), writes=[t_const])

        qT = V(46 * KB, [128, 4, 2048], BF16)
        kT = V(62 * KB, [128, 4, 2304], BF16)
        Vaug = V(80 * KB, [128, 18, 8, 65], BF16)
        hT_own = V(99 * KB, [128, 8, 2048], BF16)
        uT = V(131 * KB, [128, 4, 4096], BF16)
        xin = [V(163 * KB, [128, 1024], F32), V(167 * KB, [128, 1024], F32)]
        xs = [V(171 * KB + i * 2 * KB, [128, 1024], BF16) for i in range(4)]
        junk = V(179 * KB, [128, 1024], BF16)
        tmpT = V(181 * KB, [128, 8, 128], F32)
        hTs_x = [V(185 * KB, [128, 8, 512], BF16)]
        zq = [V(193 * KB + i * 2 * KB, [128, 512], F32) for i in range(3)]
        sq = [V(199 * KB + i * KB, [128, 512], BF16) for i in range(3)]
        rstd = [V(202 * KB + i * 2 * KB, [128, 512], F32) for i in range(2)]
        t_qT = [Tile() for _ in range(4)]
        t_kT = [Tile() for _ in range(5)]
        t_Vaug = [Tile() for _ in range(18)]
        t_hTo = [Tile() for _ in range(4)]
        t_uT = [Tile() for _ in range(8)]
        t_xin = [Tile(), Tile()]
        t_xs = [Tile() for _ in range(4)]
        t_junk, t_tmpT = Tile(), Tile()
        t_hTsx = [Tile()]
        t_zq, t_sq, t_rstd = [Tile() for _ in range(3)], [Tile() for _ in range(3)], [Tile(), Tile()]


        mm_banks = [2, 3, 4, 5]
        mm_ctr = [0]

        def next_bank():
            k = mm_banks[mm_ctr[0] % len(mm_banks)]
            mm_ctr[0] += 1
            return k

        def hts_of(Sx):
            if Sx < 4:
                return hT_own[:, :, Sx * 512:(Sx + 1) * 512], t_hTo[Sx]
            return hTs_x[0], t_hTsx[0]

        def emit_A_tile(Sx, tl):
            g = 4 * Sx + tl
            prep_A(x_d[g * 128:(g + 1) * 128, :], xin[g % 2], t_xin[g % 2], xs[tl], t_xs[tl], junk, t_junk,
                   stat[:, (g % NSTAT) * 8:(g % NSTAT) * 8 + 8], t_stat[g % NSTAT])

        def emit_B_tile(Sx, tl):
            hts, t_h = hts_of(Sx)
            g = 4 * Sx + tl
            prep_B(xs[tl], t_xs[tl], g % 2, tmpT, t_tmpT, hts[:, :, tl * 128:(tl + 1) * 128], t_h, gs1, sh1)

        def emit_B(Sx):
            for tl in range(4):
                emit_B_tile(Sx, tl)

        def proj_fm(col0, hts, t_h, n=512, t_w=None):
            t_w = [t_w] if t_w is not None else t_W
            k = next_bank()

            def mm(e):
                ins = None
                for kc in range(8):
                    ins = e.matmul(bank(k)[:, 0:n], lhsT=Wqkvu[:, kc, col0:col0 + 128],
                                   rhs=hts[:, kc, 0:n], start=(kc == 0), stop=(kc == 7))
                return ins
            S.op("pe", mm, reads=list(t_w) + [t_h], writes=[t_bank[k]])
            return k

        qk_ctr = [0]

        def qk_stage1(k, n, bias_col):
            zb = qk_ctr[0] % 3
            qk_ctr[0] += 1
            S.op("dve", lambda e: e.tensor_scalar(out=zq[zb][:, 0:n], in0=bank(k)[:, 0:n],
                                                  scalar1=bin_pp[:, bias_col:bias_col + 1], scalar2=None,
                                                  op0=ALU.add),
                 reads=[t_bank[k], t_const], writes=[t_zq[zb]])
            S.op("act", lambda e: e.activation(out=sq[zb][:, 0:n], in_=zq[zb][:, 0:n], func=AF.Square),
                 reads=[t_zq[zb]], writes=[t_sq[zb]])
            return zb

        def qk_stage2(zb, n, wcol, dst, t_dst):
            k2 = 6 + (zb % 2)
            rb = zb % 2
            S.op("pe", lambda e: e.matmul(bank(k2)[:, 0:n], lhsT=bones, rhs=sq[zb][:, 0:n], start=True, stop=True),
                 reads=[t_sq[zb], t_const], writes=[t_bank[k2]])
            S.op("act", lambda e: e.activation(out=rstd[rb][:, 0:n], in_=bank(k2)[:, 0:n], func=AF.Ln, scale=1.0,
                                               bias=eps64),
                 reads=[t_bank[k2], t_const], writes=[t_rstd[rb]])
            S.op("act", lambda e: e.activation(out=rstd[rb][:, 0:n], in_=rstd[rb][:, 0:n], func=AF.Exp, scale=-0.5),
                 reads=[t_rstd[rb]], writes=[t_rstd[rb]])
            S.op("dve", lambda e: e.scalar_tensor_tensor(out=dst, in0=zq[zb][:, 0:n], scalar=wcol,
                                                         in1=rstd[rb][:, 0:n], op0=ALU.mult, op1=ALU.mult),
                 reads=[t_zq[zb], t_rstd[rb], t_const], writes=[t_dst])

        def emit_mm(Sx, nxt):
            prefetch = nxt is not None
            hts, t_h = hts_of(Sx)
            items = []
            for idx in range(4):
                items.append(("u", idx))
                if Sx < 4:
                    items.append(("q", idx))
                if Sx < 5:
                    items.append(("k", idx))
                    if Sx < 4 or idx < 2:
                        items.append(("v", idx))
            nk = 512 if Sx < 4 else 256
            LAG = 2
            pend = {}
            npf = 0
            nbf = 0
            inter_B = prefetch and not (Sx >= 4 and nxt >= 4)
            for i in range(len(items) + LAG):
                if prefetch and npf < 4 and i >= (npf * max(len(items) - 4, 0)) // 4:
                    emit_A_tile(nxt, npf)
                    npf += 1
                if inter_B and nbf < npf and i >= (nbf * max(len(items) - 4, 0)) // 4 + 3:
                    emit_B_tile(nxt, nbf)
                    nbf += 1
                if i < len(items):
                    kind, idx = items[i]
                    if kind == "u":
                        k = proj_fm(1536 + idx * 128, hts, t_h, t_w=t_Wu)
                        S.op("dve", lambda e, k=k, gq=idx: e.tensor_scalar(
                            out=uT[:, gq, Sx * 512:(Sx + 1) * 512], in0=bank(k), scalar1=bin_pp[:, 12 + gq:13 + gq],
                            scalar2=None, op0=ALU.add),
                            reads=[t_bank[k], t_const], writes=[t_uT[Sx]])
                    elif kind == "q":
                        k = proj_fm(idx * 128, hts, t_h)
                        pend[i] = (qk_stage1(k, 512, idx), 512, qkw[:, 0:1],
                                   qT[:, idx, Sx * 512:(Sx + 1) * 512], t_qT[Sx])
                    elif kind == "k":
                        k = proj_fm(512 + idx * 128, hts, t_h, nk)
                        pend[i] = (qk_stage1(k, nk, 4 + idx), nk, kw8[:, 0:1],
                                   kT[:, idx, Sx * 512:Sx * 512 + nk], t_kT[Sx])
                    else:
                        g = 4 * Sx + idx
                        k = next_bank()

                        def mmv(e, k=k, tl=idx):
                            ins = None
                            for kc in range(8):
                                ins = e.matmul(bank(k), lhsT=hts[:, kc, tl * 128:(tl + 1) * 128],
                                               rhs=Wqkvu[:, kc, 1024:1536], start=(kc == 0), stop=(kc == 7))
                            return ins
                        S.op("pe", mmv, reads=t_W + [t_h], writes=[t_bank[k]])
                        S.op("dve", lambda e, k=k, g=g: e.tensor_copy(
                            out=Vaug[:, g, :, 0:64], in_=bank(k).rearrange("p (a b) -> p a b", a=8)),
                            reads=[t_bank[k]], writes=[t_Vaug[g]])
                j = i - LAG
                if j in pend:
                    qk_stage2(*pend.pop(j))
            while prefetch and npf < 4:
                emit_A_tile(nxt, npf)
                npf += 1
            if prefetch:
                while nbf < 4:
                    emit_B_tile(nxt, nbf)
                    nbf += 1

        order = [5, 0, 6, 1, 7, 2, 4, 3]
        for tl in range(4):
            emit_A_tile(order[0], tl)
        emit_B(order[0])
        cast_ops = []
        for kc in range(8):
            if kc % 2 == 0:
                cast_ops.append(S.op("act", lambda e, kc=kc: e.activation(out=Wqkvu[:, kc, 0:1536], in_=stg_w[kc],
                                                                          func=AF.Copy),
                                     reads=[t_stgw[kc]], writes=[t_W[0]]))
            else:
                cast_ops.append(S.op("dve", lambda e, kc=kc: e.tensor_copy(out=Wqkvu[:, kc, 0:1536], in_=stg_w[kc]),
                                     reads=[t_stgw[kc]], writes=[t_W[1]]))
        S.op("dve", lambda e: e.memset(Vaug[:, :, :, 64:65], 1.0), writes=t_Vaug, extra_deps=cast_ops)
        for i, Sx in enumerate(order):
            emit_mm(Sx, order[i + 1] if i + 1 < len(order) else None)
        S.barrier()

        btab = V(14 * KB, [128, 8, 3, 640], BF16)
        attT = V(163 * KB, [128, 4, 2048], BF16)
        NSB = 4
        tmpS = [V(179 * KB + i * 1280, [128, 640], BF16) for i in range(NSB)]
        Pb = [V(184 * KB + i * 1280, [128, 640], BF16) for i in range(NSB)]
        att = [V(191 * KB, [128, 8, 64], BF16), V(192 * KB, [128, 8, 64], BF16)]
        rec = [V(193 * KB, [128, 8], F32), V(193 * KB + 32, [128, 8], F32)]
        t_attT = [Tile() for _ in range(16)]
        t_tmpS = [Tile() for _ in range(NSB)]
        t_Pb = [Tile() for _ in range(NSB)]
        t_att = [Tile(), Tile()]
        t_rec = [Tile(), Tile()]
        t_S = [Tile(), Tile()]
        t_O = [Tile(), Tile()]
        t_tpa = Tile()
        S_ps = [PS[0], PS[1]]
        btab_v = btab.rearrange("p a b c -> p (a b c)")
        t_btab = [[Tile() for _ in range(4)] for _ in range(3)]
        btab_d4 = btab_d.rearrange("p (a b c) -> p a b c", a=8, b=3)
        def btab_load(kd, hq):
            hs = slice(2 * hq, 2 * hq + 2)
            dma("pool", btab[:, hs, kd, :], btab_d4[:, hs, kd, :], writes=[t_btab[kd][hq]])

        def btab_exp(kd, hq):
            hs = slice(2 * hq, 2 * hq + 2)
            S.op("act", lambda e: e.activation(out=btab[:, hs, kd, :], in_=btab[:, hs, kd, :], func=AF.Exp),
                 reads=[t_btab[kd][hq]], writes=[t_btab[kd][hq]])

        for kd in range(3):
            for hq in range(4):
                btab_load(kd, hq)
        for hq in range(4):
            btab_exp(0, hq)
        Ops = [PS[2][:, 0:260].rearrange("p (a b) -> p a b", a=4),
               PS[2][:, 512:772].rearrange("p (a b) -> p a b", a=4)]
        tpa = bank_bf(6)[:, 0:512].rearrange("p (a b) -> p a b", a=4)

        def att_S(lp, h, it):
            c0 = max(lp - 2, 0)
            hp, h2 = h // 2, h % 2
            pr = slice(64 * h2, 64 * h2 + 64)
            sbi = it % NSB
            spi = it % 2
            Sps = S_ps[spi]

            def mm(e):
                ins = None
                for jj in range(5):
                    ins = e.matmul(Sps[:, jj * 128:(jj + 1) * 128],
                                   lhsT=kT[pr, hp, (c0 + jj) * 128:(c0 + jj + 1) * 128],
                                   rhs=qT[pr, hp, lp * 128:(lp + 1) * 128], start=True, stop=True)
                return ins
            S.op("pe", mm, reads=t_kT + t_qT, writes=[t_S[spi]])
            kind = min(lp, 2)
            S.op("act", lambda e: e.activation(out=tmpS[sbi], in_=Sps[:, 0:640], func=AF.Exp),
                 reads=[t_S[spi]], writes=[t_tmpS[sbi]])
            S.op("dve", lambda e: e.tensor_tensor(out=Pb[sbi], in0=tmpS[sbi], in1=btab[:, h, kind, :],
                                                  op=ALU.mult),
                 reads=[t_tmpS[sbi], t_btab[kind][h // 2]], writes=[t_Pb[sbi]])

        def att_PV(lp, h, it):
            c0 = max(lp - 2, 0)
            sbi = it % NSB

            def mm(e):
                ins = None
                for jj in range(5):
                    ins = e.matmul(Ops[h // 4][:, h % 4, :], lhsT=Pb[sbi][:, jj * 128:(jj + 1) * 128],
                                   rhs=Vaug[:, c0 + jj, h, :], start=(jj == 0), stop=(jj == 4))
                return ins
            S.op("pe", mm, reads=[t_Pb[sbi]] + t_Vaug, writes=[t_O[h // 4]])

        def att_fin(lp, halves=(0, 1)):
            b = lp % 2
            for half in halves:
                S.op("dve", lambda e, half=half: e.reciprocal(out=rec[b][:, half * 4:half * 4 + 4],
                                                              in_=Ops[half][:, :, 64]),
                     reads=[t_O[half]], writes=[t_rec[b]])
                S.op("dve", lambda e, half=half: e.tensor_tensor(
                    out=att[b][:, half * 4:half * 4 + 4, :], in0=Ops[half][:, :, 0:64],
                    in1=bc(rec[b][:, half * 4:half * 4 + 4].unsqueeze(2), [128, 4, 64]), op=ALU.mult),
                    reads=[t_O[half], t_rec[b]], writes=[t_att[b]])

        def att_fin_B(lp):
            b = lp % 2
            attf = att[b].rearrange("p a b -> p (a b)")

            def tr(e):
                ins = None
                for hp in range(4):
                    ins = e.transpose(out=tpa[:, hp, :], in_=attf[:, hp * 128:(hp + 1) * 128], identity=ident)
                return ins
            S.op("pe", tr, reads=[t_att[b], t_const], writes=[t_tpa])
            S.op("dve", lambda e: e.tensor_tensor(out=attT[:, :, lp * 128:(lp + 1) * 128], in0=tpa,
                                                  in1=bc(bin_pp[:, 8:12].unsqueeze(2), [128, 4, 128]), op=ALU.add),
                 reads=[t_tpa, t_const], writes=[t_attT[lp]])

        screp2 = V(44 * KB, [128, 8, 128], BF16)
        wada2 = V(194 * KB, [128, 8, 512], BF16)
        modblk = V(202 * KB, [128, 512], F32)
        tmpd2 = V(204 * KB, [128, 4, 128], F32)
        t_screp2, t_wada2, t_modblk, t_tmpd2, t_b7 = Tile(), Tile(), Tile(), Tile(), Tile()
        S.op("dve", lambda e: e.tensor_copy(out=screp2, in_=bc(sc_sb.unsqueeze(2), [128, 8, 128])),
             reads=[t_const], writes=[t_screp2])

        def p0b_load(nb):
            dma("pool", wada2, wada_v[:, :, nb * 512:(nb + 1) * 512], writes=[t_wada2])

        def p0b_block(nb):
            sec, half = nb // 2, nb % 2

            def mm(e):
                ins = None
                for kc in range(8):
                    ins = e.matmul(bank(7), lhsT=screp2[:, kc, :], rhs=wada2[:, kc, :], start=(kc == 0),
                                   stop=(kc == 7))
                return ins
            S.op("pe", mm, reads=[t_screp2, t_wada2], writes=[t_b7])
            if sec in (2, 5):
                gbc = gate1_bc if sec == 2 else gate2_bc
                dst = gbc[:, half * 512:(half + 1) * 512]
                S.op("dve", lambda e: e.tensor_tensor(out=dst, in0=bank(7), in1=dst, op=ALU.add),
                     reads=[t_b7, t_gate], writes=[t_gate])
            else:
                i = sec - 1
                S.op("act", lambda e: e.activation(out=modblk, in_=bank(7), func=AF.Copy),
                     reads=[t_b7], writes=[t_modblk])
                S.op("dve", lambda e: e.tensor_tensor(out=tmpd2, in0=modblk.rearrange("p (a b) -> p a b", a=4),
                                                      in1=bc(identf.unsqueeze(1), [128, 4, 128]), op=ALU.mult),
                     reads=[t_modblk, t_const], writes=[t_tmpd2])
                c0_ = i * 8 + half * 4
                S.op("dve", lambda e: e.tensor_reduce(out=ppx[:, c0_:c0_ + 4], in_=tmpd2, axis=AX.X, op=ALU.add),
                     reads=[t_tmpd2], writes=[t_mods])
                S.op("dve", lambda e: e.tensor_tensor(out=ppx[:, c0_:c0_ + 4], in0=ppx[:, c0_:c0_ + 4],
                                                      in1=bada_pp[:, sec * 8 + half * 4:sec * 8 + half * 4 + 4],
                                                      op=ALU.add),
                     reads=[t_mods, t_const], writes=[t_mods])

        items = [(lp, h) for lp in range(16) for h in range(8)]
        AHEAD = 2
        for j in range(AHEAD):
            att_S(items[j][0], items[j][1], j)
        for it, (lp, h) in enumerate(items):
            if it % 2 == 0:
                for d in (AHEAD, AHEAD + 1):
                    if it + d < len(items):
                        att_S(items[it + d][0], items[it + d][1], it + d)
            att_PV(lp, h, it)
            if 2 <= it < 6:
                btab_exp(1, it - 2)
            if 9 <= it < 13:
                btab_exp(2, it - 9)
            if h == 3:
                att_fin(lp, (0,))
            if h == 7:
                att_fin(lp, (1,))
            if h == 3 and lp > 0:
                att_fin_B(lp - 1)
            if it % 14 == 0 and it // 14 < 8:
                p0b_load(4 + it // 14)
            if it % 14 == 11 and it // 14 < 8:
                p0b_block(4 + it // 14)
        att_fin_B(15)
        finish_mods(gs2, sh2, n2w, 2, 3)
        S.barrier()

        Vt = V(14 * KB, [128, 32, 1024], BF16)
        YT = V(78 * KB, [128, 4, 2048], BF16)
        dtab = [V(179 * KB, [128, 2, 4, 512], BF16), V(187 * KB, [128, 2, 4, 512], BF16)]
        t_Vt = [Tile() for _ in range(32)]
        t_YT = [Tile() for _ in range(4)]
        t_dtab = [Tile(), Tile()]
        t_PS = [Tile() for _ in range(4)]
        for g in range(32):
            vp = PS[g % 2]

            def mm(e, g=g, vp=vp):
                ins = None
                for gq in range(4):
                    ins = e.matmul(vp[:, gq * 256:(gq + 1) * 256], lhsT=uT[:, gq, g * 128:(g + 1) * 128], rhs=cs_t,
                                   start=True, stop=True)
                return ins
            last_v = S.op("pe", mm, reads=t_uT + [t_const], writes=[t_PS[g % 2]])
            if g % 2 == 0:
                S.op("act", lambda e, g=g, vp=vp: e.activation(out=Vt[:, g, :], in_=vp, func=AF.Copy),
                     reads=[t_PS[g % 2]], writes=[t_Vt[g]])
            else:
                S.op("dve", lambda e, g=g, vp=vp: e.tensor_copy(out=Vt[:, g, :], in_=vp),
                     reads=[t_PS[g % 2]], writes=[t_Vt[g]])
        Wg = V(131 * KB, [128, 8, 2048], BF16)
        Wna = V(195 * KB, [128, 4, 1024], BF16)
        t_Wg, t_Wna, t_Wfn = Tile(), Tile(), Tile()
        for kc in range(8):
            dma("pool", Wg[:, kc, :], win_v[:, kc, 2048:4096], writes=[t_Wg], extra_deps=[last_v])
        dma("pool", Wna, wna_d.rearrange("(a p) n -> p a n", p=128), writes=[t_Wna])
        t_acc = [Tile() for _ in range(4)]
        Wfn_pre = V(46 * KB, [128, 4, 1024], BF16)
        for sb in range(4):
            for sl in range(8):
                blk = sb * 8 + sl
                db = blk % 2
                dma("sp", dtab[db].rearrange("p a b c -> p (a b c)"), dft_d[blk], writes=[t_dtab[db]])

                def mm(e, sl=sl, db=db):
                    ins = None
                    for ti in range(4):
                        g = sl * 4 + ti
                        for gq in range(4):
                            e.matmul(bank(4 + gq), lhsT=Vt[:, g, gq * 256:gq * 256 + 128], rhs=dtab[db][:, 0, ti, :],
                                     start=(g == 0), stop=False)
                            ins = e.matmul(bank(4 + gq), lhsT=Vt[:, g, gq * 256 + 128:gq * 256 + 256],
                                           rhs=dtab[db][:, 1, ti, :], start=False, stop=(g == 31))
                    return ins
                pos_op = S.op("pe", mm, reads=[t_dtab[db]] + t_Vt[sl * 4:sl * 4 + 4], writes=t_acc)
                if sb == 3 and sl == 4:
                    dma("pool", Wfn_pre, wfn_d.rearrange("(a p) n -> p a n", p=128), writes=[t_Wfn],
                        extra_deps=[pos_op])
            for gq in range(4):
                if gq % 2 == 0:
                    S.op("act", lambda e, gq=gq, sb=sb: e.activation(out=YT[:, gq, sb * 512:(sb + 1) * 512],
                                                                     in_=bank(4 + gq), func=AF.Copy),
                         reads=t_acc, writes=[t_YT[sb]])
                else:
                    S.op("dve", lambda e, gq=gq, sb=sb: e.tensor_copy(out=YT[:, gq, sb * 512:(sb + 1) * 512],
                                                                      in_=bank(4 + gq)),
                         reads=t_acc, writes=[t_YT[sb]])
        S.barrier()

        mT = V(14 * KB, [128, 8, 2048], BF16)
        Wfn = V(46 * KB, [128, 4, 1024], BF16)
        sa = [V(54 * KB, [128, 512], F32), V(56 * KB, [128, 512], F32)]
        sbg = [V(58 * KB, [128, 512], F32), V(60 * KB, [128, 512], F32)]
        t1 = [V(62 * KB, [128, 512], F32), V(64 * KB, [128, 512], F32)]
        t2 = [V(66 * KB, [128, 512], F32), V(68 * KB, [128, 512], F32)]
        stgo = [V(70 * KB, [128, 1024], F32), V(74 * KB, [128, 1024], F32)]
        Wo = V(179 * KB, [128, 8, 1024], BF16)
        t_sa, t_sbg, t_t1, t_t2 = [Tile(), Tile()], [Tile(), Tile()], [Tile(), Tile()], [Tile(), Tile()]
        t_mT = [[Tile() for _ in range(4)] for _ in range(8)]
        t_Wo, t_stgo = Tile(), [Tile(), Tile()]
        wo_v = wo_d.rearrange("(kc p) n -> p kc n", p=128)
        it = 0
        n_wo = 0
        for oc in range(8):
            for tb in range(4):
                cols = slice(tb * 512, (tb + 1) * 512)
                st = it % 2
                kb = 4 * st
                it += 1

                def mm(e, oc=oc, cols=cols, kb=kb):
                    for kc in range(8):
                        e.matmul(bank(kb), lhsT=Wg[:, kc, oc * 128:(oc + 1) * 128], rhs=hT_own[:, kc, cols],
                                 start=(kc == 0), stop=(kc == 7))
                    for kc in range(8):
                        e.matmul(bank(kb + 1), lhsT=Wg[:, kc, 1024 + oc * 128:1024 + (oc + 1) * 128],
                                 rhs=hT_own[:, kc, cols], start=(kc == 0), stop=(kc == 7))
                    for hp in range(4):
                        e.matmul(bank(kb + 2), lhsT=Wna[:, hp, oc * 128:(oc + 1) * 128], rhs=attT[:, hp, cols],
                                 start=(hp == 0), stop=(hp == 3))
                    ins = None
                    for gq in range(4):
                        ins = e.matmul(bank(kb + 3), lhsT=Wfn[:, gq, oc * 128:(oc + 1) * 128], rhs=YT[:, gq, cols],
                                       start=(gq == 0), stop=(gq == 3))
                    return ins
                S.op("pe", mm, reads=[t_Wg, t_Wna, t_Wfn] + t_hTo + t_attT + t_YT,
                     writes=[t_PS[2 * st], t_PS[2 * st + 1]])
                S.op("act", lambda e, oc=oc, kb=kb, st=st: e.activation(out=sa[st], in_=bank(kb), func=AF.Sigmoid,
                                                                        bias=bin_pp[:, 16 + oc:17 + oc], scale=1.0),
                     reads=[t_PS[2 * st], t_const], writes=[t_sa[st]])
                S.op("act", lambda e, oc=oc, kb=kb, st=st: e.activation(out=sbg[st], in_=bank(kb + 1), func=AF.Sigmoid,
                                                                        bias=bin_pp[:, 24 + oc:25 + oc], scale=1.0),
                     reads=[t_PS[2 * st], t_const], writes=[t_sbg[st]])
                S.op("dve", lambda e, kb=kb, st=st: e.tensor_tensor(out=t1[st], in0=bank(kb + 2), in1=sa[st],
                                                                    op=ALU.mult),
                     reads=[t_PS[2 * st + 1], t_sa[st]], writes=[t_t1[st]])
                S.op("dve", lambda e, oc=oc, kb=kb, st=st: e.scalar_tensor_tensor(
                    out=t2[st], in0=bank(kb + 3), scalar=bfn_pp[:, oc:oc + 1], in1=sbg[st], op0=ALU.add,
                    op1=ALU.mult),
                    reads=[t_PS[2 * st + 1], t_sbg[st], t_const], writes=[t_t2[st]])
                S.op("pool", lambda e, oc=oc, cols=cols, st=st: e.tensor_tensor(out=mT[:, oc, cols], in0=t1[st],
                                                                                in1=t2[st], op=ALU.add),
                     reads=[t_t1[st], t_t2[st]], writes=[t_mT[oc][tb]])
                if it >= 3 and it % 3 == 0 and n_wo < 8:
                    i = n_wo
                    n_wo += 1
                    dma("sp", stgo[i % 2], wo_v[:, i, :], writes=[t_stgo[i % 2]])
                    S.op("pool", lambda e, i=i: e.tensor_tensor(out=Wo[:, i, :], in0=stgo[i % 2], in1=gate1_bc,
                                                                op=ALU.mult),
                         reads=[t_stgo[i % 2], t_gate], writes=[t_Wo])
        assert n_wo == 8
        S.barrier()
        xr = [V(163 * KB, [128, 1024], F32), V(167 * KB, [128, 1024], F32)]
        acc = V(99 * KB, [128, 16, 1024], F32)
        t_xr = [Tile(), Tile()]
        t_accg = [Tile() for _ in range(16)]
        W1q = [V(46 * KB, [128, 8, 1024], BF16), V(62 * KB, [128, 8, 1024], BF16)]
        W2q = [V(78 * KB, [128, 8, 1024], BF16), V(163 * KB, [128, 8, 1024], BF16)]
        stg2 = [V(195 * KB, [128, 1024], F32), V(199 * KB, [128, 1024], F32)]
        t_W1q, t_W2q = [Tile(), Tile()], [Tile(), Tile()]
        t_stg2 = [Tile(), Tile()]
        w1_v = w1_d.rearrange("(kc p) n -> p kc n", p=128)
        w2_v = w2_d.rearrange("(fc p) n -> p fc n", p=128)

        def load_w1(qt):
            b = qt % 2
            for kc in range(8):
                dma("pool", W1q[b][:, kc, :], w1_v[:, kc, qt * 1024:(qt + 1) * 1024], writes=[t_W1q[b]])

        def load_w2_piece(qt, fc):
            b = qt % 2
            i = qt * 8 + fc
            dma("sp", stg2[i % 2], w2_v[:, i, :], writes=[t_stg2[i % 2]])
            S.op("pool", lambda e: e.tensor_tensor(out=W2q[b][:, fc, :], in0=stg2[i % 2], in1=gate2_bc, op=ALU.mult),
                 reads=[t_stg2[i % 2], t_gate], writes=[t_W2q[b]])

        def load_quarter(qt):
            load_w1(qt)
            for fc in range(8):
                load_w2_piece(qt, fc)

        h2T = V(14 * KB, [128, 8, 2048], BF16)
        xs2 = [V(62 * KB + i * 2 * KB, [128, 1024], BF16) for i in range(3)]
        junk2 = V(68 * KB, [128, 1024], BF16)
        tmpT2 = V(70 * KB, [128, 8, 128], F32)
        t_h2T = [Tile() for _ in range(16)]
        t_xs2 = [Tile(), Tile(), Tile()]
        t_junk2, t_tmpT2 = Tile(), Tile()
        mm_ops = {}

        def emit_h2T(g):
            prep_B(xs2[g % 3], t_xs2[g % 3], 4 + g % 2, tmpT2, t_tmpT2, h2T[:, :, g * 128:(g + 1) * 128], t_h2T[g],
                   gs2, sh2, extra_deps=[mm_ops[g]])

        load_w1(0)
        for g in range(16):
            dma("sp", xr[g % 2], x_d[g * 128:(g + 1) * 128, :], writes=[t_xr[g % 2]])
            Cp = PS[g % 2]

            def mm(e, g=g, Cp=Cp):
                ins = None
                for half in range(2):
                    for oc in range(8):
                        ins = e.matmul(Cp[:, half * 512:(half + 1) * 512], lhsT=mT[:, oc, g * 128:(g + 1) * 128],
                                       rhs=Wo[:, oc, half * 512:(half + 1) * 512], start=(oc == 0), stop=(oc == 7))
                return ins
            mm_op = S.op("pe", mm, reads=[t_Wo] + [t_mT[oc][g // 4] for oc in range(8)], writes=[t_PS[g % 2]])
            S.op("dve", lambda e, g=g, Cp=Cp: e.tensor_tensor(out=acc[:, g, :], in0=Cp, in1=xr[g % 2], op=ALU.add),
                 reads=[t_PS[g % 2], t_xr[g % 2]], writes=[t_accg[g]])
            mm_ops[g] = mm_op
            prep_A(acc[:, g, :], None, None, xs2[g % 3], t_xs2[g % 3], junk2, t_junk2,
                   stat[:, (g % NSTAT) * 8:(g % NSTAT) * 8 + 8], t_stat[g % NSTAT], dma_src=False,
                   src_tiles=[t_accg[g]])
            if g >= 2:
                emit_h2T(g - 2)
            if g >= 8:
                load_w2_piece(0, g - 8)
        emit_h2T(14)
        emit_h2T(15)
        S.barrier()

        aTq = [V(179 * KB, [128, 8, 512], BF16), V(187 * KB, [128, 8, 512], BF16)]
        sqf = [V(94 * KB, [128, 512], F32), V(96 * KB, [128, 512], F32)]
        t_aTq = [Tile(), Tile()]
        t_sqf = [Tile(), Tile()]
        out_ops = []
        fit = [0]

        def mlp_up(qt, tb):
            b = qt % 2
            ab = (qt * 4 + tb) % 2
            for fc in range(8):
                kf = fit[0] % 2
                fit[0] += 1

                def mm(e, fc=fc, kf=kf):
                    ins = None
                    for kc in range(8):
                        ins = e.matmul(bank(kf), lhsT=W1q[b][:, kc, fc * 128:(fc + 1) * 128],
                                       rhs=h2T[:, kc, tb * 512:(tb + 1) * 512], start=(kc == 0), stop=(kc == 7))
                    return ins
                S.op("pe", mm, reads=[t_W1q[b]] + t_h2T[tb * 4:tb * 4 + 4], writes=[t_bank[kf]])
                S.op("act", lambda e, kf=kf: e.activation(out=sqf[kf], in_=bank(kf), func=AF.Square),
                     reads=[t_bank[kf]], writes=[t_sqf[kf]])
                S.op("dve", lambda e, kf=kf, fc=fc: e.scalar_tensor_tensor(
                    out=aTq[ab][:, fc, :], in0=bank(kf), scalar=0.0, in1=sqf[kf], op0=ALU.is_gt, op1=ALU.mult),
                    reads=[t_bank[kf], t_sqf[kf]], writes=[t_aTq[ab]])

        def mlp_down(qt, tb):
            b = qt % 2
            ab = (qt * 4 + tb) % 2
            for tl in range(4):
                g = tb * 4 + tl
                Dp = PS[2 + g % 2]

                def mm2(e, tl=tl, Dp=Dp):
                    ins = None
                    for half in range(2):
                        for fc in range(8):
                            ins = e.matmul(Dp[:, half * 512:(half + 1) * 512],
                                           lhsT=aTq[ab][:, fc, tl * 128:(tl + 1) * 128],
                                           rhs=W2q[b][:, fc, half * 512:(half + 1) * 512], start=(fc == 0),
                                           stop=(fc == 7))
                    return ins
                S.op("pe", mm2, reads=[t_aTq[ab], t_W2q[b]], writes=[t_PS[2 + g % 2]])
                S.op("dve", lambda e, g=g, Dp=Dp: e.tensor_tensor(out=acc[:, g, :], in0=Dp, in1=acc[:, g, :],
                                                                  op=ALU.add),
                     reads=[t_PS[2 + g % 2], t_accg[g]], writes=[t_accg[g]])
                if qt == 3:
                    out_ops.append(dma("sp", out_d[g * 128:(g + 1) * 128, :], acc[:, g, :], reads=[t_accg[g]]))

        steps = [(qt, tb) for qt in range(4) for tb in range(4)]
        load_quarter(1)
        mlp_up(*steps[0])
        for i, (qt, tb) in enumerate(steps):
            if i + 1 < len(steps):
                nq, ntb = steps[i + 1]
                if ntb == 0 and False:
                    pass
                mlp_up(nq, ntb)
            mlp_down(qt, tb)
            if tb == 3 and qt + 2 < 4:
                load_quarter(qt + 2)
        S.barrier()

        S.finalize(sems)
        with nc.Block() as block:
            @block.tensor
            def _(e):
                S.replay("pe", e)

            @block.scalar
            def _(e):
                S.replay("act", e)

            @block.vector
            def _(e):
                S.replay("dve", e)

            @block.gpsimd
            def _(e):
                S.replay("pool", e)

            @block.sync
            def _(e):
                S.replay("sp", e)
    return nc


def _rows(hf):
    return np.arange(64) if hf == 0 else np.arange(63, -1, -1)


def _bias_tables(rpb, hf):
    R = _rows(hf)
    tab = np.full((128, 8, 3, 640), NEG, np.float32)
    qcol = np.arange(64)
    kcol = np.arange(64)
    cs = np.clip(qcol - 8, 0, 48)
    colok = (kcol[:, None] >= cs[None, :]) & (kcol[:, None] <= cs[None, :] + 15)
    cidx = np.clip(kcol[:, None] - qcol[None, :] + 15, 0, 30)
    for kind in range(3):
        lp = kind
        c0 = max(lp - 2, 0)
        for jj in range(5):
            lc = c0 + jj
            for a in range(2):
                krow = R[2 * lc + a]
                for e in range(2):
                    qrow = R[2 * lp + e]
                    rs = min(max(qrow - 4, 0), 56)
                    if not (rs <= krow <= rs + 7):
                        continue
                    ridx = krow - qrow + 7
                    vals = rpb[:, ridx, :][:, cidx]
                    blk = np.where(colok[None], vals, NEG).astype(np.float32)
                    tab[a * 64:(a + 1) * 64, :, kind, jj * 128 + e * 64:jj * 128 + e * 64 + 64] = blk.transpose(1, 0, 2)
    return np.ascontiguousarray(tab.reshape(128, 8 * 3 * 640))


_DFT_CACHE = {}


def _dft_tables(hf):
    if hf in _DFT_CACHE:
        return _DFT_CACHE[hf]
    R = _rows(hf)
    sl = np.arange(4096)
    pos = 64 * R[sl // 64] + (sl % 64)
    k = np.arange(4096)
    scale = 1.0 / np.sqrt(4096.0 * 128.0)
    cosv = (np.cos(2 * np.pi * k / 4096.0) * scale)
    nsinv = (-np.sin(2 * np.pi * k / 4096.0) * scale)
    prod = (pos[:, None].astype(np.int64) * pos[None, :2048].astype(np.int64)) % 4096
    out = np.empty((32, 128, 2, 4, 512), ml_dtypes.bfloat16)
    for sb in range(4):
        for s8 in range(8):
            blk = sb * 8 + s8
            pr = prod[s8 * 512:(s8 + 1) * 512, sb * 512:(sb + 1) * 512].reshape(4, 128, 512)
            out[blk, :, 0] = cosv[pr].transpose(1, 0, 2).astype(ml_dtypes.bfloat16)
            out[blk, :, 1] = nsinv[pr].transpose(1, 0, 2).astype(ml_dtypes.bfloat16)
    res = np.ascontiguousarray(out.reshape(32, 128, 4096))
    _DFT_CACHE[hf] = res
    return res


def _pp(v, n):
    return np.ascontiguousarray(np.asarray(v, np.float32).reshape(n, 128).T)


_NC_CACHE = {}


def make_in_maps(inputs):
    f = lambda k: np.asarray(inputs[k], np.float32)
    x = f("x")
    c = f("c")
    rpb = f("rpb")[0]
    cc = np.arange(128)
    ang = 2 * np.pi * np.outer(cc, cc) / 128.0
    cst_cs = np.concatenate([np.cos(ang), np.sin(ang)], axis=1).astype(np.float32)
    bones = np.zeros((128, 128), np.float32)
    bones[:64, :64] = 1.0
    bones[64:, 64:] = 1.0
    shared = {
        "w_ada": np.ascontiguousarray(f("w_ada")[0]),
        "b_ada": np.ascontiguousarray(f("b_ada")[0].reshape(1, 6144)),
        "bada_pp": _pp(f("b_ada")[0], 48),
        "n1w_pp": _pp(f("norm1_w")[0], 8),
        "n2w_pp": _pp(f("norm2_w")[0], 8),
        "w_in": np.ascontiguousarray(f("w_in")[0]),
        "bin_pp": _pp(f("b_in")[0], 32),
        "qkw": np.ascontiguousarray(np.stack([np.tile(f("q_norm_w")[0], 2), np.tile(f("k_norm_w")[0], 2)], axis=1)),
        "w_na": np.ascontiguousarray(f("w_na_out")[0]),
        "w_fn": np.ascontiguousarray(f("w_fn_out")[0]),
        "bfn_pp": _pp(f("b_fn_out")[0], 8),
        "w_o": np.ascontiguousarray(f("w_o")[0]),
        "w1": np.ascontiguousarray(f("w_mlp_in")[0]),
        "w2": np.ascontiguousarray(f("w_mlp_out")[0]),
        "cst_ident": np.eye(128, dtype=np.float32),
        "cst_bones": bones,
        "cst_cs": cst_cs,
    }
    btabs = [_bias_tables(rpb, hf) for hf in range(2)]
    in_maps = []
    for core in range(8):
        b, hf = core // 2, core % 2
        R = _rows(hf)
        xl = np.ascontiguousarray(x[b].reshape(64, 64, 1024)[R].reshape(4096, 1024))
        m = dict(shared)
        m["x"] = xl
        m["c_pp"] = _pp(c[b], 8)
        m["btab"] = btabs[hf]
        m["dft"] = _dft_tables(hf)
        in_maps.append(m)
    return in_maps


def assemble(results, dtype=np.float32):
    out = np.empty((4, 4096, 1024), dtype)
    for core in range(8):
        b, hf = core // 2, core % 2
        R = _rows(hf)[:32]
        o = np.asarray(results[core]["out"]).reshape(32, 64, 1024)
        out[b].reshape(64, 64, 1024)[R] = o
    return out


def kernel(**inputs):
    if "nc" not in _NC_CACHE:
        _NC_CACHE["nc"] = build_program()
    nc = _NC_CACHE["nc"]
    in_maps = make_in_maps(inputs)
    res = run_bass_kernel_spmd(nc, in_maps, core_ids=list(range(8)))
    return assemble(res.results)
```
